# Optimizing a Trainium2 kernel written in Bass

```python
import math
import jax, jax.numpy as jnp
from jax import lax
import numpy as np

D_MODEL = 1024
BATCH = 2
SEQ = 8192
DEPTH = 1

N_META = 16
NORM_EPS = 1e-6
ATT_HEADS = 8
ATT_QK_DIM = 64
ATT_V_DIM = 2 * ATT_QK_DIM
ATT_QK_WIDTH = ATT_HEADS * 2 * ATT_QK_DIM
ATT_WIDTH = ATT_HEADS * ATT_V_DIM
Q_BLOCK = 128
DN_HEADS = 8
DN_K_DIM = 128
DN_V_DIM = 128
DN_KEY_WIDTH = DN_HEADS * DN_K_DIM
DN_WIDTH = DN_HEADS * DN_V_DIM
DN_CONV = 4
DN_CONV_CH = 2 * DN_KEY_WIDTH + DN_WIDTH
DN_CHUNK = 64
IN_SIZES = (ATT_QK_WIDTH, ATT_QK_WIDTH, ATT_WIDTH, ATT_WIDTH,
            DN_KEY_WIDTH, DN_KEY_WIDTH, DN_WIDTH, DN_WIDTH, DN_HEADS, DN_HEADS,
            D_MODEL, D_MODEL)
IN_DIM = sum(IN_SIZES)

kernel_name = 'hybrid_diffattn_gdn_block'


def rms_norm(x, w):
    xf = x.astype(jnp.float32)
    y = xf * lax.rsqrt(jnp.mean(xf * xf, axis=-1, keepdims=True) + NORM_EPS)
    return (y * w.astype(jnp.float32)).astype(x.dtype)


def l2_normalize(x):
    xf = x.astype(jnp.float32)
    return (xf * lax.rsqrt(jnp.sum(xf * xf, axis=-1, keepdims=True) + 1e-6)).astype(x.dtype)


def causal_depthwise_conv(x, w):
    K, C = w.shape
    return lax.conv_general_dilated(x, w[:, None, :].astype(x.dtype), window_strides=(1,),
                                    padding=[(K - 1, 0)], dimension_numbers=('NWC', 'WIO', 'NWC'),
                                    feature_group_count=C)


def diff_attention(q, k, v, lam):
    B, L, H, _, dh = q.shape
    nb = -(-L // Q_BLOCK)
    lq = nb * Q_BLOCK
    qp = jnp.pad(q, ((0, 0), (0, lq - L), (0, 0), (0, 0), (0, 0)))
    qb = qp.reshape(B, nb, Q_BLOCK, H, 2, dh).transpose(1, 0, 2, 3, 4, 5)
    k_pos = jnp.arange(L)
    scale = dh ** -0.5

    def one_block(args):
        q_blk, start = args
        s = jnp.einsum('bqhmd,bkhmd->bhmqk', q_blk, k).astype(jnp.float32) * scale
        q_pos = start + jnp.arange(Q_BLOCK)
        causal = k_pos[None, :] <= q_pos[:, None]
        p = jax.nn.softmax(jnp.where(causal, s, -jnp.inf), axis=-1)
        pd = p[:, :, 0] - lam * p[:, :, 1]
        return jnp.einsum('bhqk,bkhd->bqhd', pd.astype(v.dtype), v)

    out = lax.map(one_block, (qb, jnp.arange(nb) * Q_BLOCK))
    return out.transpose(1, 0, 2, 3, 4).reshape(B, lq, H, -1)[:, :L]


def gated_delta_chunked(q, k, v, beta, g):
    B, Lp, H, dk = q.shape
    dv = v.shape[-1]
    n = Lp // DN_CHUNK

    def chunks(t):
        return t.reshape(B, n, DN_CHUNK, H, -1).transpose(0, 3, 1, 2, 4).astype(jnp.float32)

    q, k, v = chunks(q), chunks(k), chunks(v)
    beta = chunks(beta[..., None])[..., 0]
    gc = jnp.cumsum(chunks(g[..., None])[..., 0], axis=-1)
    incl = jnp.tril(jnp.ones((DN_CHUNK, DN_CHUNK), dtype=bool))
    strict = jnp.tril(jnp.ones((DN_CHUNK, DN_CHUNK), dtype=bool), -1)
    diff = gc[..., :, None] - gc[..., None, :]
    decay = jnp.where(incl, jnp.exp(jnp.where(incl, diff, 0.0)), 0.0)
    kb = k * beta[..., None]
    m = jnp.where(strict, jnp.einsum('bhnid,bhnjd->bhnij', kb, k) * decay, 0.0)
    eye = jnp.eye(DN_CHUNK, dtype=jnp.float32)
    t_inv = lax.linalg.triangular_solve(m + eye, jnp.broadcast_to(eye, m.shape), left_side=True,
                                        lower=True, unit_diagonal=True)
    u = jnp.einsum('bhnij,bhnjd->bhnid', t_inv, v * beta[..., None])
    w = jnp.einsum('bhnij,bhnjd->bhnid', t_inv, kb * jnp.exp(gc)[..., None])
    a_intra = jnp.einsum('bhnid,bhnjd->bhnij', q, k) * decay
    q_dec = q * jnp.exp(gc)[..., None]
    k_dec = k * jnp.exp(gc[..., -1:] - gc)[..., None]
    chunk_decay = jnp.exp(gc[..., -1])

    def step(state, inp):
        u_c, w_c, a_c, qd_c, kd_c, cd_c = inp
        v_new = u_c - jnp.einsum('bhcd,bhde->bhce', w_c, state)
        o_c = jnp.einsum('bhcd,bhde->bhce', qd_c, state) + jnp.einsum('bhij,bhje->bhie', a_c, v_new)
        state = state * cd_c[..., None, None] + jnp.einsum('bhcd,bhce->bhde', kd_c, v_new)
        return state, o_c

    xs = tuple(jnp.moveaxis(t, 2, 0) for t in (u, w, a_intra, q_dec, k_dec, chunk_decay))
    s0 = jnp.zeros((B, H, dk, dv), jnp.float32)
    _, o = lax.scan(step, s0, xs)
    return o.transpose(1, 0, 3, 2, 4).reshape(B, Lp, H, dv)


def hybrid_layer(h, layer_idx, norm_w, w_in, lambda_q1, lambda_k1, lambda_q2, lambda_k2,
                 attn_norm_w, conv_w, a_log, dt_bias, dn_norm_w, w_branch_attn, w_branch_delta, w_out):
    B, L, _ = h.shape
    hn = rms_norm(h, norm_w)
    proj = hn @ w_in.astype(h.dtype)
    offs = np.cumsum(IN_SIZES)[:-1].tolist()
    aq, ak, av, az, dq, dk, dv, dz, db, da, ga, gd = jnp.split(proj, offs, axis=-1)

    lam_init = 0.8 - 0.6 * math.exp(-0.3 * layer_idx)
    f32 = jnp.float32
    lam = (jnp.exp(jnp.sum(lambda_q1.astype(f32) * lambda_k1.astype(f32)))
           - jnp.exp(jnp.sum(lambda_q2.astype(f32) * lambda_k2.astype(f32))) + lam_init)
    qa = aq.reshape(B, L, ATT_HEADS, 2, ATT_QK_DIM)
    ka = ak.reshape(B, L, ATT_HEADS, 2, ATT_QK_DIM)
    va = av.reshape(B, L, ATT_HEADS, ATT_V_DIM)
    o_a = diff_attention(qa, ka, va, lam)
    o_a = (rms_norm(o_a, attn_norm_w) * (1.0 - lam_init)).reshape(B, L, ATT_WIDTH)
    y_a = (o_a * jax.nn.silu(az)) @ w_branch_attn.astype(h.dtype)

    qkv = jnp.concatenate([dq, dk, dv], axis=-1)
    qkv = jax.nn.silu(causal_depthwise_conv(qkv, conv_w))
    cq, ck, cv = jnp.split(qkv, [DN_KEY_WIDTH, 2 * DN_KEY_WIDTH], axis=-1)
    q = l2_normalize(cq.reshape(B, L, DN_HEADS, DN_K_DIM)) * (DN_K_DIM ** -0.5)
    k = l2_normalize(ck.reshape(B, L, DN_HEADS, DN_K_DIM))
    v = cv.reshape(B, L, DN_HEADS, DN_V_DIM)
    beta = jax.nn.sigmoid(db.astype(f32))
    g = -jnp.exp(a_log.astype(f32)) * jax.nn.softplus(da.astype(f32) + dt_bias.astype(f32))
    pad = (-N_META) % DN_CHUNK

    def pad_front(t):
        return jnp.pad(t, [(0, 0), (pad, 0)] + [(0, 0)] * (t.ndim - 2))

    o_d = gated_delta_chunked(pad_front(q), pad_front(k), pad_front(v), pad_front(beta), pad_front(g))
    o_d = rms_norm(o_d[:, pad:].astype(h.dtype), dn_norm_w).reshape(B, L, DN_WIDTH)
    y_d = (o_d * jax.nn.silu(dz)) @ w_branch_delta.astype(h.dtype)

    merged = jax.nn.sigmoid(ga) * y_a + jax.nn.sigmoid(gd) * y_d
    return merged @ w_out.astype(h.dtype)


def setup_inputs(seed: int = 0) -> dict:
    key = jax.random.key(seed)
    ks = jax.random.split(key, 20)
    nrm = jax.random.normal
    dt = jnp.exp(jax.random.uniform(ks[8], (DEPTH, DN_HEADS)) * (math.log(0.1) - math.log(0.001))
                 + math.log(0.001))
    return {
        'x': nrm(ks[0], (BATCH, SEQ, D_MODEL), jnp.float32),
        'meta_tokens': nrm(ks[1], (N_META, D_MODEL), jnp.float32),
        'norm_w': 1.0 + 0.02 * nrm(ks[2], (DEPTH, D_MODEL), jnp.float32),
        'w_in': nrm(ks[3], (DEPTH, D_MODEL, IN_DIM), jnp.float32) * D_MODEL ** -0.5,
        'lambda_q1': 0.1 * nrm(ks[4], (DEPTH, ATT_QK_DIM), jnp.float32),
        'lambda_k1': 0.1 * nrm(ks[5], (DEPTH, ATT_QK_DIM), jnp.float32),
        'lambda_q2': 0.1 * nrm(ks[6], (DEPTH, ATT_QK_DIM), jnp.float32),
        'lambda_k2': 0.1 * nrm(ks[7], (DEPTH, ATT_QK_DIM), jnp.float32),
        'attn_norm_w': 1.0 + 0.02 * nrm(ks[9], (DEPTH, ATT_V_DIM), jnp.float32),
        'conv_w': nrm(ks[10], (DEPTH, DN_CONV, DN_CONV_CH), jnp.float32) * DN_CONV ** -0.5,
        'a_log': jnp.log(jax.random.uniform(ks[11], (DEPTH, DN_HEADS), jnp.float32, 1.0, 16.0)),
        'dt_bias': dt + jnp.log(-jnp.expm1(-dt)),
        'dn_norm_w': 1.0 + 0.02 * nrm(ks[12], (DEPTH, DN_V_DIM), jnp.float32),
        'w_branch_attn': nrm(ks[13], (DEPTH, ATT_WIDTH, D_MODEL), jnp.float32) * ATT_WIDTH ** -0.5,
        'w_branch_delta': nrm(ks[14], (DEPTH, DN_WIDTH, D_MODEL), jnp.float32) * DN_WIDTH ** -0.5,
        'w_out': nrm(ks[15], (DEPTH, D_MODEL, D_MODEL), jnp.float32) * D_MODEL ** -0.5,
        'final_norm_w': 1.0 + 0.02 * nrm(ks[16], (D_MODEL,), jnp.float32),
    }


def reference(x, meta_tokens, norm_w, w_in, lambda_q1, lambda_k1, lambda_q2, lambda_k2,
              attn_norm_w, conv_w, a_log, dt_bias, dn_norm_w, w_branch_attn, w_branch_delta,
              w_out, final_norm_w):
    B = x.shape[0]
    meta = jnp.broadcast_to(meta_tokens[None].astype(x.dtype), (B, N_META, D_MODEL))
    h = jnp.concatenate([meta, x], axis=1)
    for l in range(DEPTH):
        h = h + hybrid_layer(h, l, norm_w[l], w_in[l], lambda_q1[l], lambda_k1[l], lambda_q2[l],
                             lambda_k2[l], attn_norm_w[l], conv_w[l], a_log[l], dt_bias[l],
                             dn_norm_w[l], w_branch_attn[l], w_branch_delta[l], w_out[l])
    return rms_norm(h, final_norm_w)[:, N_META:]
```

```python
import math
import numpy as np
import ml_dtypes
from contextlib import ExitStack
import concourse.bass as bass
import concourse.mybir as mybir
from concourse.bass_utils import run_bass_kernel_spmd

F32 = mybir.dt.float32
BF16 = mybir.dt.bfloat16
AF = mybir.ActivationFunctionType
ALU = mybir.AluOpType
AX = mybir.AxisListType
NPBF = ml_dtypes.bfloat16

D = 1024
NMETA = 16
EPS = 1e-6
LAM_INIT = 0.8 - 0.6 * math.exp(-0.3 * 0)
ENGS = ("pe", "act", "dve", "pool", "sp")
SAME_ENGINE_SYNC = True


class Stream:
    def __init__(self, name, step):
        self.name, self.step, self.sem, self.nops, self.ninc = name, step, None, 0, 0


class Op:
    __slots__ = ("issuer", "stream", "emit", "waits", "inc", "idx", "value", "clock", "ext")

    def __init__(self, issuer, stream, emit):
        self.issuer, self.stream, self.emit = issuer, stream, emit
        self.waits, self.inc, self.idx, self.value, self.clock = [], False, 0, 0, None
        self.ext = ()


class Prog:
    def __init__(self, nc, tag="s"):
        self.nc = nc
        self.tag = tag
        self.streams = {e: Stream(e, 1) for e in ENGS}
        self.ops = {e: [] for e in ENGS}
        self.known = {e: {} for e in ENGS}
        self.last_writer, self.readers, self.all_ops = {}, {}, []

    def op(self, issuer, emit, reads=(), writes=(), stream=None, after=(), ext=()):
        if stream is None:
            st = self.streams[issuer]
        else:
            st = self.streams.setdefault(stream, Stream(stream, 1 if stream.startswith("cc") else 16))
        o = Op(issuer, st, emit)
        o.ext = tuple(ext)
        if st.step == 1 and st.name.startswith("cc"):
            o.inc = True
        st.nops += 1
        o.idx = st.nops
        deps = list(after)
        for r in reads:
            w = self.last_writer.get(r)
            if w is not None:
                deps.append(w)
        for w_ in writes:
            w = self.last_writer.get(w_)
            if w is not None:
                deps.append(w)
            deps.extend(self.readers.get(w_, ()))
        known = self.known[issuer]
        best = {}
        for d in deps:
            sname = d.stream.name
            if sname == issuer and (issuer == "pe" or not SAME_ENGINE_SYNC):
                continue
            if known.get(sname, 0) >= d.idx:
                continue
            b = best.get(sname)
            if b is None or d.idx > b.idx:
                best[sname] = d
        for d in best.values():
            o.waits.append(d)
            d.inc = True
            for k, v in d.clock.items():
                if known.get(k, 0) < v:
                    known[k] = v
        clock = dict(known)
        clock[st.name] = max(clock.get(st.name, 0), o.idx)
        o.clock = clock
        for r in reads:
            self.readers.setdefault(r, []).append(o)
        for w_ in writes:
            self.last_writer[w_] = o
            self.readers[w_] = []
        self.ops[issuer].append(o)
        self.all_ops.append(o)
        return o

    def final_wait(self, issuer, ops):
        o = Op(issuer, self.streams[issuer], lambda e: None)
        for d in ops:
            o.waits.append(d)
            d.inc = True
        o.clock = {}
        self.ops[issuer].append(o)

    def barrier(self):
        last = {}
        for o in self.all_ops:
            if not o.stream.name.startswith("cc"):
                last[o.stream.name] = o
        for e in ENGS:
            o = Op(e, self.streams[e], lambda eng: None)
            for d in last.values():
                o.waits.append(d)
                d.inc = True
            o.clock = {}
            self.ops[e].append(o)

    def finalize(self, stack, semstack=None):
        nc = self.nc
        semstack = semstack or stack
        for o in self.all_ops:
            if o.inc:
                s = o.stream
                s.ninc += 1
                o.value = s.ninc * s.step
        for s in self.streams.values():
            if s.ninc > 0:
                s.sem = semstack.enter_context(nc.semaphore(self.tag + "_" + s.name.replace(":", "_")))
        block = stack.enter_context(nc.Block())
        prog = self

        def replay(eng, key):
            for o in prog.ops[key]:
                for d in o.waits:
                    eng.wait_ge(d.stream.sem, d.value)
                for (sem_, val_) in o.ext:
                    eng.wait_ge(sem_, val_)
                ins = o.emit(eng)
                if o.inc:
                    if o.stream.step == 1 and o.stream.name.startswith("cc"):
                        ins.then_inc(o.stream.sem)
                    else:
                        ins.then_inc(o.stream.sem, o.stream.step)

        block.tensor(lambda e: replay(e, "pe"))
        block.scalar(lambda e: replay(e, "act"))
        block.vector(lambda e: replay(e, "dve"))
        block.gpsimd(lambda e: replay(e, "pool"))
        block.sync(lambda e: replay(e, "sp"))


def build_fused(NSB):
    NT = NSB * 512
    NKT = 1 + NSB * 4
    NSB2 = NSB // 4
    NT2 = NSB2 * 512
    nc = bass.Bass("TRN2", target_bir_lowering=False)
    dt_in = lambda n, s, d=F32: nc.dram_tensor(n, s, d, kind="ExternalInput").ap()
    xT2 = dt_in("xT2", [D, NT2])
    xtok = dt_in("xtok", [NT2, D])
    wg = dt_in("wg", [D, 2048])
    wba = dt_in("wba", [D, D])
    wbd = dt_in("wbd", [D, D])
    wo = dt_in("wo", [D, D])
    fnw = dt_in("fnw", [1, D])
    y = nc.dram_tensor("y", [NT2, D], F32, kind="ExternalOutput").ap()
    cinA_t = [nc.dram_tensor("cinA%d" % c, [256, NT2 // 2], F32) for c in range(4)]
    cinD_t = [nc.dram_tensor("cinD%d" % c, [256, NT2 // 2], F32) for c in range(4)]
    coutA_t = nc.dram_tensor("coutA", [4 * 1024, NT2 // 2], F32)
    coutD_t = nc.dram_tensor("coutD", [4 * 1024, NT2 // 2], F32)
    cinA = [t.ap() for t in cinA_t]
    cinD = [t.ap() for t in cinD_t]
    coutA, coutD = coutA_t.ap(), coutD_t.ap()
    GROUPS = [[0, 1, 2, 3], [4, 5, 6, 7]]
    ownA = nc.dram_tensor("ownA", [1024, NT2 // 2], F32).ap()
    ownD = nc.dram_tensor("ownD", [1024, NT2 // 2], F32).ap()
    semstack = ExitStack()
    xT = dt_in("xT", [D, NT])
    metaT = dt_in("metaT", [D, 128])
    wfm = dt_in("wfm", [D, 1792])
    wbg = dt_in("wbg", [D, 4])
    wav = dt_in("wav", [D, 256])
    normw = dt_in("normw", [128, 8])
    convw = dt_in("convw", [128, 24])
    alog = dt_in("alog", [1, 2])
    dtb = dt_in("dtb", [1, 2])
    lamv = dt_in("lamv", [1, 256])
    anw = dt_in("anw", [128, 1])
    dnw = dt_in("dnw", [128, 1])
    consts = dt_in("consts", [128, 1024])

    with ExitStack() as st:
        sb_ = lambda n, s, d=F32: st.enter_context(nc.sbuf_tensor(n, s, d))
        W_fm = sb_("W_fm", [128, 8, 2304], BF16)
        W_av = sb_("W_av", [128, 8, 256], BF16)
        KT = sb_("KT", [128, 2, NMETA + NT], BF16)
        V1 = sb_("V1", [128, 2, NKT, 129], BF16)
        x32 = sb_("x32", [128, 8, 512])
        sq = sb_("sq", [128, 2, 512], BF16)
        rstd = sb_("rstd", [128, 512])
        hnT = sb_("hnT", [128, 8, 512], BF16)
        QT = sb_("QT", [128, 2, 512], BF16)
        saz = sb_("saz", [128, 2, 512], BF16)
        sdz2 = sb_("sdz", [128, 2, 512], BF16)
        pre = sb_("pre", [128, 3, 515])
        cv = sb_("cv", [128, 3, 512])
        hist = sb_("hist", [128, 6, 3])
        beta_b = sb_("beta_b", [128, 512])
        g_b = sb_("g_b", [128, 512])
        gc_b = sb_("gc_b", [128, 512])
        egc2 = sb_("egc_b", [128, 2, 512])
        ekd_b = sb_("ekd_b", [128, 512])
        tmpf = sb_("tmpf", [128, 512])
        qnT = sb_("qnT", [128, 512], BF16)
        knT = sb_("knT", [128, 512], BF16)
        qdT2 = sb_("qdT", [128, 2, 512], BF16)
        kbgT = sb_("kbgT", [128, 512], BF16)
        kdT = sb_("kdT", [128, 512], BF16)
        vbT = sb_("vbT", [128, 512], BF16)
        PT = sb_("PT", [128, 2, 2, 512], BF16)
        ozA_s = sb_("ozA_s", [128, 2, 512], BF16)
        ozD2 = sb_("ozD_s", [128, 2, 512], BF16)
        cst = sb_("cst", [128, 1024])
        identb = sb_("identb", [128, 128], BF16)
        onesb = sb_("onesb", [128, 128], BF16)
        causb = sb_("causb", [128, 128], BF16)
        nw = sb_("nw", [128, 8])
        cw = sb_("cw", [128, 24])
        wbg32 = sb_("wbg32", [128, 8, 4])
        sm = sb_("sm", [128, 32])
        lamt = sb_("lamt", [128, 256])
        lamp = sb_("lamp", [128, 128])
        S32 = sb_("S32", [128, 2, 128])
        Sb = sb_("Sb", [128, 2, 128], BF16)
        junk = sb_("junk", [128, 128])
        junk2 = sb_("junk2", [128, 128])
        MA = sb_("MA", [128, 4, 256], BF16)
        kdm = sb_("kdm", [128, 4, 128], BF16)
        um = sb_("um", [128, 4, 128])
        wTm = sb_("wTm", [128, 4, 128], BF16)
        fz = sb_("fz", [128, 2])
        gcol = sb_("gcol", [128, 4])
        zer = sb_("zer", [128, 128])
        vnew = sb_("vnew", [128, 128], BF16)
        ot = sb_("ot", [128, 128])
        r0t = sb_("r0t", [128, 512])
        r1t = sb_("r1t", [128, 512])
        oat = sb_("oat", [128, 512])
        accs = sb_("accs", [128, 2, 512])
        ones32 = sb_("ones32", [128, 128])
        onb = sb_("onb", [128, 128], BF16)
        ss = sb_("ss", [128, 8])
        SA = st.enter_context(nc.psum_tensor("SA", [128, 2, 512], F32))
        SB = st.enter_context(nc.psum_tensor("SB", [128, 2, 512], F32))
        ACC = [st.enter_context(nc.psum_tensor("ACC%d" % i, [128, 512], F32)) for i in range(3)]
        B = [SA[:, 0, :], SA[:, 1, :], SB[:, 0, :], ACC[0][:], ACC[1][:], ACC[2][:], SB[:, 1, :]]
        SBANK = [(SA, (0, 1)), (SB, (2, 6))]
        pTr = st.enter_context(nc.psum_tensor("pTr", [128, 1024], BF16))

        identf = cst[:, 0:128]
        inclT = cst[:, 128:256]
        strictT = cst[:, 256:384]
        resetm = cst[:, 512:1024]
        P = Prog(nc, "p1")
        sink = [None]

        def op(issuer, emit, reads=(), writes=(), stream=None, after=()):
            if sink[0] is not None:
                sink[0].append((issuer, emit, tuple(reads), tuple(writes)))
                return None
            return P.op(issuer, emit, reads=reads, writes=writes, stream=stream, after=after)

        def emit_list(lst, extra_reads=()):
            for it_ in lst:
                if it_[0] == "dma":
                    _, q_, o_, i_, r_, w_ = it_
                    dma(o_, i_, reads=tuple(r_) + tuple(extra_reads), writes=w_, q=q_)
                else:
                    (i_, e_, r_, w_) = it_
                    P.op(i_, e_, reads=tuple(r_) + tuple(extra_reads), writes=w_)

        TX = "x32tok"

        def fence():
            op("pool", lambda e: e.memset(fz[:, 0:1], 0.0), writes=[TX])

        def interleave(a_, b_):
            out_, ia, ib = [], 0, 0
            while ia < len(a_) or ib < len(b_):
                if ib >= len(b_) or (ia < len(a_) and ia * len(b_) <= ib * len(a_)):
                    out_.append(a_[ia]); ia += 1
                else:
                    out_.append(b_[ib]); ib += 1
            return out_
        dq = [0]

        dlast = {}

        def dma(out, in_, reads=(), writes=(), q=None):
            q = q or "sp"
            if sink[0] is not None:
                sink[0].append(("dma", q, out, in_, tuple(reads), tuple(writes)))
                return None
            dq[0] += 1
            name = "%s:%d" % (q, dq[0] % 8)
            prev = dlast.get(name)
            o_ = op(q, lambda e, o=out, i=in_: e.dma_start(out=o, in_=i), reads=reads, writes=writes, stream=name, after=([prev] if prev is not None else ()))
            dlast[name] = o_
            return o_

        def mm(out, lhsT, rhs, start=True, stop=True, reads=(), writes=()):
            return op("pe", lambda e, o=out, l=lhsT, r=rhs, s0=start, s1=stop: e.matmul(o, lhsT=l, rhs=r, start=s0, stop=s1, skip_group_check=True), reads=reads, writes=writes)

        def tr(out, in_, reads=(), writes=()):
            return op("pe", lambda e, o=out, i=in_: e.transpose(o, i, identb[:]), reads=list(reads) + ["identb"], writes=writes)

        def act(out, in_, func, scale=1.0, bias=None, accum=None, reads=(), writes=()):
            def f(e, o=out, i=in_, fn=func, s=scale, b=bias, a=accum):
                kw = {}
                if b is not None:
                    kw["bias"] = b
                if a is not None:
                    kw["accum_out"] = a
                return e.activation(out=o, in_=i, func=fn, scale=s, **kw)
            return op("act", f, reads=reads, writes=writes)

        def ve(eng, fn, reads=(), writes=()):
            return op(eng, fn, reads=reads, writes=writes)

        def tt(eng, out, a, b, alu, reads=(), writes=()):
            return op(eng, lambda e, o=out, x=a, y=b, u=alu: e.tensor_tensor(out=o, in0=x, in1=y, op=u), reads=reads, writes=writes)

        def ts(eng, out, a, s1, s2, op0, op1=None, reads=(), writes=()):
            def f(e, o=out, x=a, p=s1, q=s2, u=op0, v=op1):
                if v is None:
                    return e.tensor_scalar(out=o, in0=x, scalar1=p, scalar2=None, op0=u)
                return e.tensor_scalar(out=o, in0=x, scalar1=p, scalar2=q, op0=u, op1=v)
            return op(eng, f, reads=reads, writes=writes)

        def stt(eng, out, a, s, b, op0, op1, reads=(), writes=()):
            return op(eng, lambda e, o=out, x=a, p=s, y=b, u=op0, v=op1: e.scalar_tensor_tensor(out=o, in0=x, scalar=p, in1=y, op0=u, op1=v), reads=reads, writes=writes)

        dma(cst[:], consts[:, :], writes=["cst"])
        dma(nw[:], normw[:, :], writes=["nw"])
        dma(cw[:], convw[:, :], writes=["cw"])
        dma(wbg32[:], wbg.rearrange("(k p) c -> p k c", p=128), writes=["wbg32"])
        dma(sm[:, 4:6], alog.partition_broadcast(128), writes=["sm_a"])
        dma(sm[:, 6:8], dtb.partition_broadcast(128), writes=["sm_d"])
        dma(sm[:, 8:9], anw[:, :], writes=["sm_anw"])
        dma(sm[:, 9:10], dnw[:, :], writes=["sm_dnw"])
        dma(lamt[:], lamv.partition_broadcast(128), writes=["lamt"])
        ve("dve", lambda e: e.tensor_copy(out=identb[:], in_=identf), reads=["cst"], writes=["identb"])
        ve("dve", lambda e: e.tensor_copy(out=causb[:], in_=cst[:, 384:512]), reads=["cst"], writes=["causb"])
        ve("dve", lambda e: e.memset(onesb[:], 1.0), writes=["onesb"])
        ve("dve", lambda e: e.memset(ones32[:], 1.0), writes=["ones32"])
        ve("pool", lambda e: e.memset(V1[:, :, :, 128:129], 1.0), writes=["V1"])
        ve("pool", lambda e: e.memset(hist[:], 0.0), writes=["hist"])
        ve("pool", lambda e: e.memset(S32[:], 0.0), writes=["S32_0", "S32_1"])
        ve("pool", lambda e: e.memset(Sb[:], 0.0), writes=["Sb_0", "Sb_1"])
        ve("pool", lambda e: e.memset(zer[:], 0.0), writes=["zer"])
        ve("pool", lambda e: e.memset(vnew[:], 0.0), writes=["vnew"])
        ve("pool", lambda e: e.memset(ot[:], 0.0), writes=["ot"])
        tt("dve", lamp[:, 0:64], lamt[:, 0:64], lamt[:, 64:128], ALU.mult, reads=["lamt"], writes=["lamp"])
        tt("dve", lamp[:, 64:128], lamt[:, 128:192], lamt[:, 192:256], ALU.mult, reads=["lamt"], writes=["lamp"])
        ve("dve", lambda e: e.reduce_sum(out=sm[:, 2:3], in_=lamp[:, 0:64], axis=AX.X), reads=["lamp"], writes=["sm_s"])
        ve("dve", lambda e: e.reduce_sum(out=sm[:, 3:4], in_=lamp[:, 64:128], axis=AX.X), reads=["lamp"], writes=["sm_s"])
        act(sm[:, 2:4], sm[:, 2:4], AF.Exp, reads=["sm_s"], writes=["sm_s"])
        tt("dve", sm[:, 0:1], sm[:, 2:3], sm[:, 3:4], ALU.subtract, reads=["sm_s"], writes=["sm_lam"])
        ts("dve", sm[:, 1:2], sm[:, 0:1], -1.0, -LAM_INIT, ALU.mult, ALU.add, reads=["sm_lam"], writes=["sm_lam"])
        ts("dve", sm[:, 8:9], sm[:, 8:9], 1.0 - LAM_INIT, None, ALU.mult, reads=["sm_anw"], writes=["sm_anw"])
        act(sm[:, 4:6], sm[:, 4:6], AF.Exp, reads=["sm_a"], writes=["sm_a"])
        ts("dve", sm[:, 4:6], sm[:, 4:6], -1.0, None, ALU.mult, reads=["sm_a"], writes=["sm_a"])
        x32f = x32[:].rearrange("p a b -> p (a b)")
        for kc in range(8):
            hb = (kc % 2) * 2048
            sk = [("x", 4 * (kc % 2) + i) for i in range(4)]
            dma(x32f[:, hb:hb + 1792], wfm[kc * 128:(kc + 1) * 128, :], writes=sk)
            dma(x32f[:, hb + 1792:hb + 2048], wav[kc * 128:(kc + 1) * 128, :], reads=sk, writes=[("wavd", kc)], q="pool")
            ts("dve", W_fm[:, kc, 0:1792], x32f[:, hb:hb + 1792], nw[:, kc:kc + 1], None, ALU.mult, reads=sk + ["nw"], writes=["W_fm"])
            act(W_av[:, kc, :], x32f[:, hb + 1792:hb + 2048], AF.Copy, scale=nw[:, kc:kc + 1], reads=sk + ["nw", ("wavd", kc)], writes=["W_av"])
            for j in range(4):
                ts("dve", W_fm[:, kc, 1792 + j * 128:1920 + j * 128], wbg32[:, kc, j:j + 1].to_broadcast([128, 128]), nw[:, kc:kc + 1], None, ALU.mult, reads=["wbg32", "nw"], writes=["W_fm"])
        bank = [0]

        def nextbank():
            bank[0] ^= 1
            return bank[0]

        def inproj_fm(c, T):
            b = nextbank()
            for kc in range(8):
                mm(B[b][:, 0:T], W_fm[:, kc, c * 128:(c + 1) * 128], hnT[:, kc, 0:T], start=(kc == 0), stop=(kc == 7), reads=["W_fm", "hnT"], writes=[("B", b)])
            return b

        def superblock(sbi, prevRC):
            meta = sbi < 0
            T = 128 if meta else 512
            src = metaT if meta else xT
            c0 = 0 if meta else sbi * 512
            sink[0] = []
            fence()
            for kc in range(8):
                dma(x32[:, kc, 0:T], src[kc * 128:(kc + 1) * 128, c0:c0 + T], reads=[TX], writes=[("x", kc)], q=("sp" if kc % 2 == 0 else "pool"))
            for kc in range(8):
                act(sq[:, kc % 2, 0:T], x32[:, kc, 0:T], AF.Square, reads=[("x", kc), TX], writes=[("sq", kc % 2)])
                mm(B[2][:, 0:T], onesb[:], sq[:, kc % 2, 0:T], start=(kc == 0), stop=(kc == 7), reads=["onesb", ("sq", kc % 2)], writes=[("B", 2)])
            ts("dve", rstd[:, 0:T], B[2][:, 0:T], 1.0 / D, EPS, ALU.mult, ALU.add, reads=[("B", 2)], writes=["rstd"])
            act(rstd[:, 0:T], rstd[:, 0:T], AF.Ln, reads=["rstd"], writes=["rstd"]); act(rstd[:, 0:T], rstd[:, 0:T], AF.Exp, scale=-0.5, reads=["rstd"], writes=["rstd"])
            for kc in range(8):
                tt("dve" if kc % 2 == 0 else "pool", hnT[:, kc, 0:T], x32[:, kc, 0:T], rstd[:, 0:T], ALU.mult, reads=[("x", kc), "rstd", TX], writes=["hnT"])
            for h in range(2):
                koff = 0 if meta else NMETA + sbi * 512
                if not meta:
                    b = inproj_fm(0 + h, T)
                    act(QT[:, h, 0:T], B[b][:, 0:T], AF.Copy, reads=[("B", b)], writes=[("QT", h)])
                b = inproj_fm(2 + h, T)
                if meta:
                    ve("dve", lambda e, b=b, h=h: e.tensor_copy(out=KT[:, h, 0:16], in_=B[b][:, 112:128]), reads=[("B", b)], writes=[("KT", h, -1)])
                else:
                    ve("dve", lambda e, b=b, h=h, k=koff: e.tensor_copy(out=KT[:, h, k:k + 512], in_=B[b][:, 0:512]), reads=[("B", b)], writes=[("KT", h, sbi)])
                if not meta:
                    b = inproj_fm(4 + h, T)
                    act(saz[:, h, 0:T], B[b][:, 0:T], AF.Silu, reads=[("B", b)], writes=[("saz", h)])
            ntile = 1 if meta else 4
            for j in range(ntile):
                b = nextbank()
                for kc in range(8):
                    lh = hnT[:, kc, 112:128] if meta else hnT[:, kc, j * 128:(j + 1) * 128]
                    np_ = 16 if meta else 128
                    mm(B[b][0:np_, 0:256], lh, W_av[:, kc, :], start=(kc == 0), stop=(kc == 7), reads=["hnT", "W_av"], writes=[("B", b)])
                kt = 0 if meta else 1 + sbi * 4 + j
                for h in range(2):
                    np_ = 16 if meta else 128
                    ve("dve", lambda e, b=b, h=h, kt=kt, n=np_: e.tensor_copy(out=V1[0:n, h, kt, 0:128], in_=B[b][0:n, h * 128:(h + 1) * 128]), reads=[("B", b)], writes=[("V1", h, kt)])
            AB = sink[0]
            sink[0] = None
            emit_list(interleave(AB, prevRC))
            if sbi > 0 and sbi % NSB2 == 0:
                collect(sbi // NSB2 - 1)
            EPI1 = []
            if not meta:
                for h in range(2):
                    nkt = 1 + 4 * sbi + 4

                    def att_qk(kt):
                        nk = 16 if kt == 0 else 128
                        ks = 0 if kt == 0 else NMETA + (kt - 1) * 128
                        a = kt - 1 - 4 * sbi
                        diag = a >= 0
                        qlo = a * 128 if diag else 0
                        buf = kt % 2
                        ksb = -1 if kt == 0 else (kt - 1) // 4
                        Sps, (k0, k1) = SBANK[buf]
                        for m in range(2):
                            mm(Sps[0:nk, m, qlo:512], KT[m * 64:(m + 1) * 64, h, ks:ks + nk], QT[m * 64:(m + 1) * 64, h, qlo:512], reads=[("KT", h, ksb), ("QT", h)], writes=[("B", (k0, k1)[m])])
                        act(PT[0:nk, buf, :, qlo:512], Sps[0:nk, :, qlo:512], AF.Exp, scale=0.125, reads=[("B", k0), ("B", k1)], writes=[("PT", buf, 0), ("PT", buf, 1)])
                        for m in range(2):
                            if diag:
                                tt("pool", PT[:, buf, m, qlo:qlo + 128], PT[:, buf, m, qlo:qlo + 128], causb[:], ALU.mult, reads=[("PT", buf, m), "causb"], writes=[("PT", buf, m)])

                    def att_pv(kt):
                        nk = 16 if kt == 0 else 128
                        a = kt - 1 - 4 * sbi
                        qlo = a * 128 if a >= 0 else 0
                        buf = kt % 2
                        tt("dve", accs[0:nk, :, qlo:512], accs[0:nk, :, qlo:512], PT[0:nk, buf, :, qlo:512], ALU.add, reads=["accs", ("PT", buf, 0), ("PT", buf, 1)], writes=["accs"])
                        for m in range(2):
                            mm(B[3 + m][:, qlo:512], V1[0:nk, h, kt, 0:128], PT[0:nk, buf, m, qlo:512], start=(kt == 0), stop=(kt == nkt - 1), reads=[("PT", buf, m), ("V1", h, kt)], writes=[("B", 3 + m)])

                    ve("pool", lambda e: e.memset(accs[:], 0.0), writes=["accs"])
                    for step in range(nkt + 1):
                        if step < nkt:
                            att_qk(step)
                        if step >= 1:
                            att_pv(step - 1)
                    if h == 1:
                        sink[0] = []
                    for m, rt in ((0, r0t), (1, r1t)):
                        mm(B[5][:, 0:512], ones32[:], accs[:, m, :], reads=["ones32", "accs"], writes=[("B", 5)])
                        ve("dve", lambda e, rt=rt: e.reciprocal(out=rt[:], in_=B[5][:, 0:512]), reads=[("B", 5)], writes=[("rt", m)])
                    ts("dve", r1t[:], r1t[:], sm[:, 1:2], None, ALU.mult, reads=[("rt", 1), "sm_lam"], writes=[("rt", 1)])
                    tt("dve", oat[:], B[3][:, 0:512], r0t[:], ALU.mult, reads=[("B", 3), ("rt", 0)], writes=["oat"])
                    tt("dve", r1t[:], B[4][:, 0:512], r1t[:], ALU.mult, reads=[("B", 4), ("rt", 1)], writes=[("rt", 1)])
                    tt("dve", oat[:], oat[:], r1t[:], ALU.add, reads=["oat", ("rt", 1)], writes=["oat"])
                    act(sq[:, 0, :], oat[:], AF.Square, reads=["oat"], writes=[("sq", 0)])
                    mm(B[5][:, 0:512], onesb[:], sq[:, 0, :], reads=["onesb", ("sq", 0)], writes=[("B", 5)])
                    ts("dve", r0t[:], B[5][:, 0:512], 1.0 / 128, EPS, ALU.mult, ALU.add, reads=[("B", 5)], writes=[("rt", 0)])
                    act(r0t[:], r0t[:], AF.Ln, reads=[("rt", 0)], writes=[("rt", 0)])
                    act(r0t[:], r0t[:], AF.Exp, scale=-0.5, reads=[("rt", 0)], writes=[("rt", 0)])
                    tt("dve", oat[:], oat[:], r0t[:], ALU.mult, reads=["oat", ("rt", 0)], writes=["oat"])
                    stt("dve", ozA_s[:, h, :], oat[:], sm[:, 8:9], saz[:, h, :], ALU.mult, ALU.mult, reads=["oat", "sm_anw", ("saz", h)], writes=[("ozA_s", h)])
                    dma(cinA[sbi // NSB2][h * 128:(h + 1) * 128, (sbi % NSB2) * 256:(sbi % NSB2) * 256 + 256], ozA_s[:, h, :].bitcast(F32), reads=[("ozA_s", h)], writes=[("cinA", h, sbi)])
                    if h == 1:
                        EPI1 = sink[0]
                        sink[0] = None
            Wl, PRl, RCl = [], [], []
            for h in range(2):
                sdz, egc_b, qdT, ozD_s = sdz2[:, h, :], egc2[:, h, :], qdT2[:, h, :], ozD2[:, h, :]
                kE, kQ, kZ, kO = ("egc_b", h), ("qdT", h), ("sdz", h), ("ozD_s", h)
                sink[0] = []
                for j in range(3):
                    b = inproj_fm(6 + 2 * j + h, T)
                    ve("pool", lambda e, j=j, h=h: e.tensor_copy(out=pre[:, j, 0:3], in_=hist[:, h * 3 + j, :]), reads=["hist"], writes=[("pre", j)])
                    act(pre[:, j, 3:3 + T], B[b][:, 0:T], AF.Copy, reads=[("B", b)], writes=[("pre", j)])
                if not meta:
                    b = inproj_fm(12 + h, T)
                    act(sdz[:, 0:T], B[b][:, 0:T], AF.Silu, reads=[("B", b)], writes=[kZ])
                b = inproj_fm(14 + h, T)
                act(beta_b[:, 0:T], B[b][:, 0:T], AF.Sigmoid, reads=[("B", b)], writes=["beta_b"])
                b = inproj_fm(16 + h, T)
                act(g_b[:, 0:T], B[b][:, 0:T], AF.Exp, bias=sm[:, 6 + h:7 + h], reads=[("B", b), "sm_d"], writes=["g_b"])
                act(g_b[:, 0:T], g_b[:, 0:T], AF.Ln, bias=1.0, reads=["g_b"], writes=["g_b"])
                for j in range(3):
                    w = (2 * j + h) * 4
                    ts("dve", cv[:, j, 0:T], pre[:, j, 0:T], cw[:, w:w + 1], None, ALU.mult, reads=[("pre", j), "cw"], writes=[("cv", j)])
                    for tap in range(1, 4):
                        stt("dve", cv[:, j, 0:T], pre[:, j, tap:tap + T], cw[:, w + tap:w + tap + 1], cv[:, j, 0:T], ALU.mult, ALU.add, reads=[("pre", j), "cw", ("cv", j)], writes=[("cv", j)])
                    ve("pool", lambda e, j=j, h=h, T=T: e.tensor_copy(out=hist[:, h * 3 + j, :], in_=pre[:, j, T:T + 3]), reads=[("pre", j)], writes=["hist"])
                ts("dve", g_b[:, 0:T], g_b[:, 0:T], sm[:, 4 + h:5 + h], None, ALU.mult, reads=["g_b", "sm_a"], writes=["g_b"])
                ve("dve", lambda e, T=T: e.tensor_tensor_scan(out=gc_b[:, 0:T], data0=resetm[:, 0:T], data1=g_b[:, 0:T], initial=0.0, op0=ALU.mult, op1=ALU.add), reads=["cst", "g_b"], writes=["gc_b"])
                for j in range(3):
                    act(cv[:, j, 0:T], cv[:, j, 0:T], AF.Silu, reads=[("cv", j)], writes=[("cv", j)])
                act(egc_b[:, 0:T], gc_b[:, 0:T], AF.Exp, reads=["gc_b"], writes=[kE])
                scr = (tmpf, ekd_b)
                bnk = (2, 6)
                for j in range(2):
                    act(sq[:, j, 0:T], cv[:, j, 0:T], AF.Square, reads=[("cv", j)], writes=[("sq", j)])
                for j in range(2):
                    mm(B[bnk[j]][:, 0:T], onesb[:], sq[:, j, 0:T], reads=["onesb", ("sq", j)], writes=[("B", bnk[j])])
                for j in range(2):
                    act(scr[j][:, 0:T], B[bnk[j]][:, 0:T], AF.Ln, bias=1e-6, reads=[("B", bnk[j])], writes=[("scr", j)])
                for j in range(2):
                    act(scr[j][:, 0:T], scr[j][:, 0:T], AF.Exp, scale=-0.5, reads=[("scr", j)], writes=[("scr", j)])
                for j, dst in ((0, qnT), (1, knT)):
                    stt("dve", dst[:, 0:T], cv[:, j, 0:T], (128.0 ** -0.5) if j == 0 else 1.0, scr[j][:, 0:T], ALU.mult, ALU.mult, reads=[("cv", j), ("scr", j)], writes=["qnT" if j == 0 else "knT"])
                tt("dve", vbT[:, 0:T], cv[:, 2, 0:T], beta_b[:, 0:T], ALU.mult, reads=[("cv", 2), "beta_b"], writes=["vbT"])
                tt("dve", qdT[:, 0:T], qnT[:, 0:T], egc_b[:, 0:T], ALU.mult, reads=["qnT", kE], writes=[kQ])
                tt("pool", tmpf[:, 0:T], beta_b[:, 0:T], egc_b[:, 0:T], ALU.mult, reads=["beta_b", kE, ("scr", 0)], writes=[("scr", 0)])
                tt("pool", kbgT[:, 0:T], knT[:, 0:T], tmpf[:, 0:T], ALU.mult, reads=["knT", ("scr", 0)], writes=["kbgT"])
                nch = T // 64
                for ch in range(nch):
                    ts("dve", ekd_b[:, ch * 64:(ch + 1) * 64], gc_b[:, ch * 64:(ch + 1) * 64], gc_b[:, ch * 64 + 63:ch * 64 + 64], None, ALU.subtract, reads=["gc_b", ("scr", 1)], writes=[("scr", 1)])
                act(ekd_b[:, 0:T], ekd_b[:, 0:T], AF.Exp, scale=-1.0, reads=[("scr", 1)], writes=[("scr", 1)])
                tt("pool", kdT[:, 0:T], knT[:, 0:T], ekd_b[:, 0:T], ALU.mult, reads=["knT", ("scr", 1)], writes=["kdT"])
                Wl.append(sink[0])
                sink[0] = None
                Sk, Sbk = "S32_%d" % h, "Sb_%d" % h

                def prep(p):
                    pc = slice(p * 128, (p + 1) * 128)
                    s_ = p % 4
                    bi_ = (0, 1, 2, 6)[s_]
                    Bp, bk = B[bi_], ("B", bi_)
                    xb_ = s_ * 1024
                    Eb_s, EE_s, junkp_s = x32f[:, xb_:xb_ + 128], x32f[:, xb_ + 128:xb_ + 384], x32f[:, xb_ + 384:xb_ + 512]
                    bfv = x32f[:, xb_ + 512:xb_ + 1024].bitcast(BF16)
                    PPv = lambda c_: bfv[:, c_ * 256:(c_ + 1) * 256]
                    Ybv = lambda c_: bfv[:, 512 + c_ * 128:512 + (c_ + 1) * 128]
                    kbgm_s, vbm_s = bfv[:, 768:896], bfv[:, 896:1024]
                    k = lambda n: (n, s_)
                    ra, rb = slice(2 * s_ * 128, (2 * s_ + 1) * 128), slice((2 * s_ + 1) * 128, (2 * s_ + 2) * 128)
                    ka, kb = "pTr%d" % (2 * s_), "pTr%d" % (2 * s_ + 1)
                    Es = Eb_s
                    MTs, ATs = MA[:, s_, 0:128], MA[:, s_, 128:256]
                    tt("dve", junkp_s, gc_b[:, pc], identf, ALU.mult, reads=["gc_b", "cst"], writes=[k("junkp")])
                    ve("dve", lambda e, s_=s_, j_=junkp_s: e.reduce_sum(out=gcol[:, s_:s_ + 1], in_=j_, axis=AX.X), reads=[k("junkp")], writes=[k("gcol")])
                    stt("dve", Es, gc_b[:, pc], gcol[:, s_:s_ + 1], zer[:], ALU.subtract, ALU.min, reads=["gc_b", k("gcol"), "zer"], writes=[k("E")])
                    act(Es, Es, AF.Exp, reads=[k("E")], writes=[k("E")])
                    tt("pool", EE_s[:, 128:256], Es, inclT, ALU.mult, reads=[k("E"), "cst"], writes=[k("EI")])
                    tt("pool", EE_s[:, 0:128], Es, strictT, ALU.mult, reads=[k("E"), "cst"], writes=[k("EB")])
                    tt("pool", EE_s[:, 0:128], EE_s[:, 0:128], beta_b[:, pc], ALU.mult, reads=[k("EB"), "beta_b"], writes=[k("EB")])
                    mm(Bp[:, 0:128], knT[:, pc], knT[:, pc], reads=["knT"], writes=[bk])
                    mm(Bp[:, 128:256], knT[:, pc], qnT[:, pc], reads=["knT", "qnT"], writes=[bk])
                    tt("dve", MA[:, s_, :], Bp[:, 0:256], EE_s, ALU.mult, reads=[bk, k("EB"), k("EI")], writes=[k("MA")])
                    tr(pTr[:, ra], MTs, reads=[k("MA")], writes=[ka, "pTrbank"])
                    act(PPv(0)[:, 0:128], pTr[:, ra], AF.Copy, reads=["pTrbank", ka], writes=[("PP", s_, 0)])
                    tt("dve", Ybv(0), identf, MTs, ALU.subtract, reads=["cst", k("MA")], writes=[("Yb", s_, 0)])
                    cur = 0
                    for lvl in range(1, 6):
                        nx = cur ^ 1
                        Pc = PPv(cur)[:, 0:128]
                        PTc = MTs if lvl == 1 else PPv(cur)[:, 128:256]
                        rk = [("PP", s_, cur), k("MA")]
                        mm(Bp[:, 0:128], PTc, Pc, reads=rk, writes=[bk])
                        if lvl < 5:
                            mm(Bp[:, 128:256], Pc, PTc, reads=rk, writes=[bk])
                            act(PPv(nx), Bp[:, 0:256], AF.Copy, reads=[bk], writes=[("PP", s_, nx)])
                        else:
                            act(PPv(nx)[:, 0:128], Bp[:, 0:128], AF.Copy, reads=[bk], writes=[("PP", s_, nx)])
                        mm(Bp[:, 256:384], PPv(nx)[:, 0:128], Ybv(cur), reads=[("PP", s_, nx), ("Yb", s_, cur)], writes=[bk])
                        tt("dve", Ybv(nx), Bp[:, 256:384], Ybv(cur), ALU.add, reads=[bk, ("Yb", s_, cur)], writes=[("Yb", s_, nx)])
                        cur = nx
                    assert cur == 1
                    Y, Yk = Ybv(1), ("Yb", s_, 1)
                    tr(pTr[:, rb], kbgT[:, pc], reads=["kbgT"], writes=[kb, "pTrbank"])
                    act(kbgm_s, pTr[:, rb], AF.Copy, reads=["pTrbank", kb], writes=[k("kbgm")])
                    tr(pTr[:, rb], vbT[:, pc], reads=["vbT"], writes=[kb, "pTrbank"])
                    ve("dve", lambda e, v_=vbm_s, rb=rb: e.tensor_copy(out=v_, in_=pTr[:, rb]), reads=["pTrbank", kb], writes=[k("vbm")])
                    tr(pTr[:, ra], kdT[:, pc], reads=["kdT"], writes=[ka, "pTrbank"])
                    act(kdm[:, s_, :], pTr[:, ra], AF.Copy, reads=["pTrbank", ka], writes=[k("kdm")])
                    mm(Bp[:, 384:512], Y, vbm_s, reads=[Yk, k("vbm")], writes=[bk])
                    act(um[:, s_, :], Bp[:, 384:512], AF.Copy, reads=[bk], writes=[k("um")])
                    mm(Bp[:, 256:384], kbgm_s, Y, reads=[Yk, k("kbgm")], writes=[bk])
                    ve("dve", lambda e, s_=s_, Bp=Bp: e.tensor_copy(out=wTm[:, s_, :], in_=Bp[:, 256:384]), reads=[bk], writes=[k("wTm")])

                def rec(p):
                    pc = slice(p * 128, (p + 1) * 128)
                    s_ = p % 4
                    ra = slice(2 * s_ * 128, (2 * s_ + 1) * 128)
                    ka = "pTr%d" % (2 * s_)
                    k = lambda n: (n, s_)
                    for c in ((1,) if meta else (0, 1)):
                        hs = slice(c * 64, (c + 1) * 64)
                        mm(B[3][:, 0:128], wTm[:, s_, :], Sb[:, h, :], reads=[k("wTm"), Sbk], writes=[("B", 3)])
                        tt("dve", vnew[hs, :], um[hs, s_, :], B[3][hs, 0:128], ALU.subtract, reads=[k("um"), ("B", 3)], writes=["vnew"])
                        mm(B[4][:, 0:128], qdT[:, pc], Sb[:, h, :], start=True, stop=False, reads=[kQ, Sbk], writes=[("B", 4)])
                        mm(B[4][:, 0:128], MA[:, s_, 128:256], vnew[:], start=False, stop=True, reads=[k("MA"), "vnew"], writes=[("B", 4)])
                        act(ot[hs, :], B[4][hs, 0:128], AF.Copy, reads=[("B", 4)], writes=["ot"])
                        mm(B[5][:, 0:128], kdm[hs, s_, :], vnew[hs, :], reads=[k("kdm"), "vnew"], writes=[("B", 5)])
                        cdc = p * 128 + c * 64 + 63
                        stt("dve", Sb[:, h, :], S32[:, h, :], egc_b[:, cdc:cdc + 1], B[5][:, 0:128], ALU.mult, ALU.add, reads=[Sk, kE, ("B", 5)], writes=[Sbk])
                        stt("dve", S32[:, h, :], S32[:, h, :], egc_b[:, cdc:cdc + 1], B[5][:, 0:128], ALU.mult, ALU.add, reads=[Sk, kE, ("B", 5)], writes=[Sk])
                    if not meta:
                        ve("pool", lambda e: e.memset(ss[:, 5:6], 0.0), writes=["ss5"])
                        act(junk2[:], ot[:], AF.Square, accum=ss[:, 5:6], reads=["ot"], writes=["junk2", "ss5"])
                        ts("dve", ss[:, 5:6], ss[:, 5:6], 1.0 / 128, EPS, ALU.mult, ALU.add, reads=["ss5"], writes=["ss5"])
                        act(ss[:, 5:6], ss[:, 5:6], AF.Ln, reads=["ss5"], writes=["ss5"]); act(ss[:, 5:6], ss[:, 5:6], AF.Exp, scale=-0.5, reads=["ss5"], writes=["ss5"])
                        ts("dve", onb[:], ot[:], ss[:, 5:6], None, ALU.mult, reads=["ot", "ss5"], writes=["onb"])
                        tr(pTr[:, ra], onb[:], reads=["onb"], writes=[ka, "pTrbank"])
                        stt("dve", ozD_s[:, pc], pTr[:, ra], sm[:, 9:10], sdz[:, pc], ALU.mult, ALU.mult, reads=["pTrbank", ka, "sm_dnw", kZ], writes=[kO])

                npairs = T // 128
                preps, recs = [], []
                for p in range(npairs):
                    sink[0] = []
                    prep(p)
                    preps.append(sink[0])
                    sink[0] = []
                    rec(p)
                    recs.append(sink[0])
                sink[0] = None
                lst = []
                for l_ in preps:
                    lst = interleave(lst, l_)
                PRl.append(lst)
                rc_ = []
                for p in range(npairs):
                    rc_ = rc_ + recs[p]
                if not meta:
                    sink[0] = []
                    dma(cinD[sbi // NSB2][h * 128:(h + 1) * 128, (sbi % NSB2) * 256:(sbi % NSB2) * 256 + 256], ozD_s.bitcast(F32), reads=[kO], writes=[("cinD", h, sbi)])
                    rc_ = rc_ + sink[0]
                    sink[0] = None
                RCl.append(rc_)
            emit_list(interleave(EPI1, Wl[0]))
            fence()
            emit_list(PRl[0], extra_reads=(TX,))
            emit_list(interleave(RCl[0], Wl[1]))
            fence()
            emit_list(PRl[1], extra_reads=(TX,))
            return RCl[1]

        def collect(c):
            kA = [("cinA", h, sj) for h in range(2) for sj in range(c * NSB2, (c + 1) * NSB2)]
            kD = [("cinD", h, sj) for h in range(2) for sj in range(c * NSB2, (c + 1) * NSB2)]
            op("pool", lambda e, c=c: e.collective_compute("AllGather", ALU.bypass, replica_groups=GROUPS, ins=[cinA_t[c].ap().opt()], outs=[coutA[c * 1024:(c + 1) * 1024, :].opt()]), reads=kA, writes=[("coutA", c)], stream="ccA")
            op("pool", lambda e, c=c: e.collective_compute("AllGather", ALU.bypass, replica_groups=GROUPS, ins=[cinD_t[c].ap().opt()], outs=[coutD[c * 1024:(c + 1) * 1024, :].opt()]), reads=kD, writes=[("coutD", c)], stream="ccD")

        pend = superblock(-1, [])
        for sbi in range(NSB):
            pend = superblock(sbi, pend)
        emit_list(pend)
        collect(NSB // NSB2 - 1)
        P.barrier()
        P.finalize(st, semstack)
        P1 = P

    NT = NT2


    with ExitStack() as st:
        sb_ = lambda n, s, d=F32: st.enter_context(nc.sbuf_tensor("q_" + n, s, d))
        Wg = sb_("Wg", [128, 8, 2048], BF16)
        Wba = sb_("Wba", [128, 8, D], BF16)
        Wbd = sb_("Wbd", [128, 8, D], BF16)
        Wo = sb_("Wo", [128, 8, D], BF16)
        x32 = sb_("x32", [128, 8, 512])
        sq = sb_("sq", [128, 2, 512], BF16)
        rstd = sb_("rstd", [128, 512])
        hnT = sb_("hnT", [128, 8, 512], BF16)
        oA = sb_("oA", [128, 8, 512], BF16)
        oD = sb_("oD", [128, 8, 512], BF16)
        sg = sb_("sg", [128, 16, 512])
        tA = sb_("tA", [128, 512])
        mT = sb_("mT", [128, 8, 512], BF16)
        xt = sb_("xt", [128, D])
        h2 = sb_("h2", [128, D])
        yo = sb_("yo", [128, D])
        junk = sb_("junk", [128, D])
        fw = sb_("fw", [128, D])
        nw = sb_("nw", [128, 8])
        onesb = sb_("onesb", [128, 128], BF16)
        ss = sb_("ss", [128, 4])
        B = [st.enter_context(nc.psum_tensor("QB%d" % i, [128, 512], F32)) for i in range(6)]
        P = Prog(nc, "p2")
        op = P.op
        dq = [0]

        dlast = {}

        def dma(out, in_, reads=(), writes=(), q=None):
            q = q or "sp"
            dq[0] += 1
            name = "%s:%d" % (q, dq[0] % 8)
            prev = dlast.get(name)
            o_ = op(q, lambda e, o=out, i=in_: e.dma_start(out=o, in_=i), reads=reads, writes=writes, stream=name, after=([prev] if prev is not None else ()))
            dlast[name] = o_
            return o_

        def mm(out, lhsT, rhs, start=True, stop=True, reads=(), writes=()):
            return op("pe", lambda e, o=out, l=lhsT, r=rhs, s0=start, s1=stop: e.matmul(o, lhsT=l, rhs=r, start=s0, stop=s1, skip_group_check=True), reads=reads, writes=writes)

        def act(out, in_, func, scale=1.0, accum=None, reads=(), writes=()):
            def f(e, o=out, i=in_, fn=func, s=scale, a=accum):
                kw = {}
                if a is not None:
                    kw["accum_out"] = a
                return e.activation(out=o, in_=i, func=fn, scale=s, **kw)
            return op("act", f, reads=reads, writes=writes)

        def tt(eng, out, a, b, alu, reads=(), writes=()):
            return op(eng, lambda e, o=out, x=a, y_=b, u=alu: e.tensor_tensor(out=o, in0=x, in1=y_, op=u), reads=reads, writes=writes)

        def ts(eng, out, a, s1, s2, op0, op1=None, reads=(), writes=()):
            def f(e, o=out, x=a, p=s1, q=s2, u=op0, v=op1):
                if v is None:
                    return e.tensor_scalar(out=o, in0=x, scalar1=p, scalar2=None, op0=u)
                return e.tensor_scalar(out=o, in0=x, scalar1=p, scalar2=q, op0=u, op1=v)
            return op(eng, f, reads=reads, writes=writes)

        def stt(eng, out, a, s, b, op0, op1, reads=(), writes=()):
            return op(eng, lambda e, o=out, x=a, p=s, y_=b, u=op0, v=op1: e.scalar_tensor_tensor(out=o, in0=x, scalar=p, in1=y_, op0=u, op1=v), reads=reads, writes=writes)

        dma(nw[:], normw[:, :], writes=["nw"])
        dma(fw[:], fnw.partition_broadcast(128), writes=["fw"])
        op("dve", lambda e: e.memset(onesb[:], 1.0), writes=["onesb"])
        x32f = x32[:].rearrange("p a b -> p (a b)")
        n = 0
        sgf = sg[:].rearrange("p a b -> p (a b)")
        for (src, dst, ncol, fold) in ((wg, Wg, 2048, True), (wba, Wba, D, False), (wbd, Wbd, D, False), (wo, Wo, D, False)):
            for kc in range(8):
                for c in range(0, ncol, 2048):
                    w_ = min(2048, ncol - c)
                    sl_ = n % 6
                    if sl_ < 2:
                        stg_, hb = x32f, sl_ * 2048
                        sk = [("x", 4 * sl_ + i) for i in range(4)]
                    else:
                        stg_, hb = sgf, (sl_ - 2) * 2048
                        sk = [("sg", 4 * (sl_ - 2) + i) for i in range(4)]
                    dma(stg_[:, hb:hb + w_], src[kc * 128:(kc + 1) * 128, c:c + w_], writes=sk, q=("sp", "pool", "act")[n % 3])
                    if n % 2 == 0:
                        if fold:
                            ts("dve", dst[:, kc, c:c + w_], stg_[:, hb:hb + w_], nw[:, kc:kc + 1], None, ALU.mult, reads=sk + ["nw"], writes=[id(dst)])
                        else:
                            op("dve", lambda e, d_=dst, kc=kc, c=c, w_=w_, hb=hb, stg_=stg_: e.tensor_copy(out=d_[:, kc, c:c + w_], in_=stg_[:, hb:hb + w_]), reads=sk, writes=[id(dst)])
                    else:
                        sc_ = nw[:, kc:kc + 1] if fold else 1.0
                        op("act", lambda e, d_=dst, kc=kc, c=c, w_=w_, hb=hb, sc_=sc_, stg_=stg_: e.activation(out=d_[:, kc, c:c + w_], in_=stg_[:, hb:hb + w_], func=AF.Copy, scale=sc_), reads=sk + ["nw"], writes=[id(dst)])
                    n += 1
        outs = []
        pidc = {}

        def rq(e):
            if 'r' not in pidc:
                pidc['r'] = e.partition_id() % 4
            return pidc['r']

        ccw = [(P1.streams[n_].sem, P1.streams[n_].ninc) for n_ in ("ccA", "ccD")]
        op("sp", lambda e: e.dma_start(out=ownA[:, :], in_=coutA[bass.ts(rq(e), 1024), :]), writes=["ownA"], stream="g:0", ext=ccw)
        op("sp", lambda e: e.dma_start(out=ownD[:, :], in_=coutD[bass.ts(rq(e), 1024), :]), writes=["ownD"], stream="g:1", ext=ccw)
        for s in range(NSB2):
            c0 = s * 512
            for kc in range(8):
                dma(x32[:, kc, :], xT2[kc * 128:(kc + 1) * 128, c0:c0 + 512], writes=[("x", kc)], q=("sp" if kc % 2 == 0 else "pool"))
                dma(oA[:, kc, :].bitcast(F32), ownA[kc * 128:(kc + 1) * 128, s * 256:(s + 1) * 256], reads=["ownA"], writes=[("oA", kc)])
                dma(oD[:, kc, :].bitcast(F32), ownD[kc * 128:(kc + 1) * 128, s * 256:(s + 1) * 256], reads=["ownD"], writes=[("oD", kc)], q="pool")
            for kc in range(8):
                act(sq[:, kc % 2, :], x32[:, kc, :], AF.Square, reads=[("x", kc)], writes=[("sq", kc % 2)])
                mm(B[0][:, :], onesb[:], sq[:, kc % 2, :], start=(kc == 0), stop=(kc == 7), reads=["onesb", ("sq", kc % 2)], writes=[("B", 0)])
            ts("dve", rstd[:], B[0][:, :], 1.0 / D, EPS, ALU.mult, ALU.add, reads=[("B", 0)], writes=["rstd"])
            act(rstd[:], rstd[:], AF.Ln, reads=["rstd"], writes=["rstd"]); act(rstd[:], rstd[:], AF.Exp, scale=-0.5, reads=["rstd"], writes=["rstd"])
            for kc in range(8):
                tt("dve" if kc % 2 == 0 else "pool", hnT[:, kc, :], x32[:, kc, :], rstd[:], ALU.mult, reads=[("x", kc), "rstd"], writes=["hnT"])
            for c in range(16):
                b = c % 2
                for kc in range(8):
                    mm(B[b][:, :], Wg[:, kc, c * 128:(c + 1) * 128], hnT[:, kc, :], start=(kc == 0), stop=(kc == 7), reads=[id(Wg), "hnT"], writes=[("B", b)])
                act(sg[:, c, :], B[b][:, :], AF.Sigmoid, reads=[("B", b)], writes=[("sg", c)])
            for dc in range(8):
                ba_, bd_ = (2, 3) if dc % 2 == 0 else (4, 5)
                for kc in range(8):
                    mm(B[ba_][:, :], Wba[:, kc, dc * 128:(dc + 1) * 128], oA[:, kc, :], start=(kc == 0), stop=(kc == 7), reads=[id(Wba), ("oA", kc)], writes=[("B", ba_)])
                for kc in range(8):
                    mm(B[bd_][:, :], Wbd[:, kc, dc * 128:(dc + 1) * 128], oD[:, kc, :], start=(kc == 0), stop=(kc == 7), reads=[id(Wbd), ("oD", kc)], writes=[("B", bd_)])
                tt("dve", tA[:], B[ba_][:, :], sg[:, dc, :], ALU.mult, reads=[("B", ba_), ("sg", dc)], writes=["tA"])
                tt("dve", junk[:, 0:512], B[bd_][:, :], sg[:, 8 + dc, :], ALU.mult, reads=[("B", bd_), ("sg", 8 + dc)], writes=["junk5"])
                tt("dve", mT[:, dc, :], junk[:, 0:512], tA[:], ALU.add, reads=["junk5", "tA"], writes=["mT"])
            for j in range(4):
                r0 = c0 + j * 128
                dma(xt[:], xtok[r0:r0 + 128, :], writes=["xt"])
                for hc in range(2):
                    bo_ = (j % 2) * 2 + hc
                    for dc in range(8):
                        mm(B[bo_][:, :], mT[:, dc, j * 128:(j + 1) * 128], Wo[:, dc, hc * 512:(hc + 1) * 512], start=(dc == 0), stop=(dc == 7), reads=["mT", id(Wo)], writes=[("B", bo_)])
                    tt("dve", h2[:, hc * 512:(hc + 1) * 512], B[bo_][:, :], xt[:, hc * 512:(hc + 1) * 512], ALU.add, reads=[("B", bo_), "xt"], writes=["h2"])
                op("pool", lambda e: e.memset(ss[:, 0:1], 0.0), writes=["ss0"])
                act(junk[:], h2[:], AF.Square, accum=ss[:, 0:1], reads=["h2"], writes=["junk5", "ss0"])
                ts("dve", ss[:, 0:1], ss[:, 0:1], 1.0 / D, EPS, ALU.mult, ALU.add, reads=["ss0"], writes=["ss0"])
                act(ss[:, 0:1], ss[:, 0:1], AF.Ln, reads=["ss0"], writes=["ss0"]); act(ss[:, 0:1], ss[:, 0:1], AF.Exp, scale=-0.5, reads=["ss0"], writes=["ss0"])
                stt("dve", yo[:], h2[:], ss[:, 0:1], fw[:], ALU.mult, ALU.mult, reads=["h2", "ss0", "fw"], writes=["yo"])
                outs.append(dma(y[r0:r0 + 128, :], yo[:], reads=["yo"]))
        P.final_wait("sp", outs)
        P.finalize(st, semstack)
    semstack.close()
    return nc


def _consts():
    c = np.zeros((128, 1024), np.float32)
    j = np.arange(128)[:, None]
    i = np.arange(128)[None, :]
    same = (j // 64) == (i // 64)
    c[:, 0:128] = np.eye(128, dtype=np.float32)
    c[:, 128:256] = ((i >= j) & same)
    c[:, 256:384] = ((i > j) & same)
    c[:, 384:512] = (i >= j)
    c[:, 512:1024] = (np.arange(512) % 64 != 0)[None, :]
    return c


def kernel(x, meta_tokens, norm_w, w_in, lambda_q1, lambda_k1, lambda_q2, lambda_k2,
           attn_norm_w, conv_w, a_log, dt_bias, dn_norm_w, w_branch_attn, w_branch_delta,
           w_out, final_norm_w):
    f = lambda a: np.ascontiguousarray(np.asarray(a, dtype=np.float32))
    x = f(x)
    Bn, S, _ = x.shape
    NSB = S // 512
    w = f(w_in)[0]
    offs = np.cumsum([0, 1024, 1024, 1024, 1024, 1024, 1024, 1024, 1024, 8, 8, 1024, 1024])
    seg = lambda i: w[:, offs[i]:offs[i + 1]]
    aq, ak, av, az, dq_, dk_, dv_, dz, db, da, ga, gd = [seg(i) for i in range(12)]
    cwf = f(conv_w)[0]
    nwl = f(norm_w)[0].reshape(8, 128).T.copy()
    metaT = np.zeros((D, 128), np.float32)
    metaT[:, 112:] = f(meta_tokens).T
    lamv = np.concatenate([f(lambda_q1)[0], f(lambda_k1)[0], f(lambda_q2)[0], f(lambda_k2)[0]])[None, :].copy()
    consts = _consts()
    xTs = [np.ascontiguousarray(x[b].T) for b in range(Bn)]
    maps1 = []
    for core in range(8):
        b, hp = core // 4, core % 4
        hs = [2 * hp, 2 * hp + 1]
        col = lambda m, h: m[:, h * 128:(h + 1) * 128]
        wfm = np.concatenate([col(m, h) for m in (aq, ak, az, dq_, dk_, dv_, dz) for h in hs], axis=1)
        wbgl = np.stack([db[:, hs[0]], db[:, hs[1]], da[:, hs[0]], da[:, hs[1]]], axis=1)
        wavl = np.concatenate([col(av, h) for h in hs], axis=1)
        cwl = np.zeros((128, 24), np.float32)
        for j in range(3):
            for hh in range(2):
                ch0 = j * 1024 + hs[hh] * 128
                cwl[:, (2 * j + hh) * 4:(2 * j + hh) * 4 + 4] = cwf[:, ch0:ch0 + 128].T
        maps1.append(dict(
            xT=xTs[b], metaT=metaT, wfm=np.ascontiguousarray(wfm), wbg=np.ascontiguousarray(wbgl),
            wav=np.ascontiguousarray(wavl), normw=nwl, convw=cwl,
            alog=f(a_log)[0][hs][None, :].copy(), dtb=f(dt_bias)[0][hs][None, :].copy(), lamv=lamv,
            anw=f(attn_norm_w)[0][:, None].copy(), dnw=f(dn_norm_w)[0][:, None].copy(), consts=consts))
    NQ = S // 4
    wgl = np.ascontiguousarray(np.concatenate([ga, gd], axis=1))
    for core in range(8):
        b, r = core // 4, core % 4
        sl = slice(r * NQ, (r + 1) * NQ)
        maps1[core].update(dict(
            xT2=np.ascontiguousarray(xTs[b][:, sl]), xtok=np.ascontiguousarray(x[b, sl]),
            wg=wgl, wba=f(w_branch_attn)[0], wbd=f(w_branch_delta)[0], wo=f(w_out)[0],
            fnw=f(final_norm_w)[None, :].copy()))
    nc = build_fused(NSB)
    res = run_bass_kernel_spmd(nc, maps1, core_ids=list(range(8))).results
    out = np.zeros((Bn, S, D), np.float32)
    for core in range(8):
        b, r = core // 4, core % 4
        out[b, r * NQ:(r + 1) * NQ] = np.asarray(res[core]["y"])
    return out
```

```python
import math
import numpy as np
import ml_dtypes
from contextlib import ExitStack
import concourse.bass as bass
import concourse.mybir as mybir
from concourse.bass_utils import run_bass_kernel_spmd

F32 = mybir.dt.float32
BF16 = mybir.dt.bfloat16
AF = mybir.ActivationFunctionType
ALU = mybir.AluOpType
AX = mybir.AxisListType
NPBF = ml_dtypes.bfloat16

D = 1024
NMETA = 16
EPS = 1e-6
LAM_INIT = 0.8 - 0.6 * math.exp(-0.3 * 0)
ENGS = ("pe", "act", "dve", "pool", "sp")
SAME_ENGINE_SYNC = True


class Stream:
    def __init__(self, name, step):
        self.name, self.step, self.sem, self.nops, self.ninc = name, step, None, 0, 0


class Op:
    __slots__ = ("issuer", "stream", "emit", "waits", "inc", "idx", "value", "clock", "ext")

    def __init__(self, issuer, stream, emit):
        self.issuer, self.stream, self.emit = issuer, stream, emit
        self.waits, self.inc, self.idx, self.value, self.clock = [], False, 0, 0, None
        self.ext = ()


class Prog:
    def __init__(self, nc, tag="s"):
        self.nc = nc
        self.tag = tag
        self.streams = {e: Stream(e, 1) for e in ENGS}
        self.ops = {e: [] for e in ENGS}
        self.known = {e: {} for e in ENGS}
        self.last_writer, self.readers, self.all_ops = {}, {}, []

    def op(self, issuer, emit, reads=(), writes=(), stream=None, after=(), ext=()):
        if stream is None:
            st = self.streams[issuer]
        else:
            st = self.streams.setdefault(stream, Stream(stream, 1 if stream.startswith("cc") else 16))
        o = Op(issuer, st, emit)
        o.ext = tuple(ext)
        if st.step == 1 and st.name.startswith("cc"):
            o.inc = True
        st.nops += 1
        o.idx = st.nops
        deps = list(after)
        for r in reads:
            w = self.last_writer.get(r)
            if w is not None:
                deps.append(w)
        for w_ in writes:
            w = self.last_writer.get(w_)
            if w is not None:
                deps.append(w)
            deps.extend(self.readers.get(w_, ()))
        known = self.known[issuer]
        best = {}
        for d in deps:
            sname = d.stream.name
            if sname == issuer and (issuer == "pe" or not SAME_ENGINE_SYNC):
                continue
            if known.get(sname, 0) >= d.idx:
                continue
            b = best.get(sname)
            if b is None or d.idx > b.idx:
                best[sname] = d
        for d in best.values():
            o.waits.append(d)
            d.inc = True
            for k, v in d.clock.items():
                if known.get(k, 0) < v:
                    known[k] = v
        clock = dict(known)
        clock[st.name] = max(clock.get(st.name, 0), o.idx)
        o.clock = clock
        for r in reads:
            self.readers.setdefault(r, []).append(o)
        for w_ in writes:
            self.last_writer[w_] = o
            self.readers[w_] = []
        self.ops[issuer].append(o)
        self.all_ops.append(o)
        return o

    def final_wait(self, issuer, ops):
        o = Op(issuer, self.streams[issuer], lambda e: None)
        for d in ops:
            o.waits.append(d)
            d.inc = True
        o.clock = {}
        self.ops[issuer].append(o)

    def barrier(self):
        last = {}
        for o in self.all_ops:
            if not o.stream.name.startswith("cc"):
                last[o.stream.name] = o
        for e in ENGS:
            o = Op(e, self.streams[e], lambda eng: None)
            for d in last.values():
                o.waits.append(d)
                d.inc = True
            o.clock = {}
            self.ops[e].append(o)

    def finalize(self, stack, semstack=None):
        nc = self.nc
        semstack = semstack or stack
        for o in self.all_ops:
            if o.inc:
                s = o.stream
                s.ninc += 1
                o.value = s.ninc * s.step
        for s in self.streams.values():
            if s.ninc > 0:
                s.sem = semstack.enter_context(nc.semaphore(self.tag + "_" + s.name.replace(":", "_")))
        block = stack.enter_context(nc.Block())
        prog = self

        def replay(eng, key):
            for o in prog.ops[key]:
                for d in o.waits:
                    eng.wait_ge(d.stream.sem, d.value)
                for (sem_, val_) in o.ext:
                    eng.wait_ge(sem_, val_)
                ins = o.emit(eng)
                if o.inc:
                    if o.stream.step == 1 and o.stream.name.startswith("cc"):
                        ins.then_inc(o.stream.sem)
                    else:
                        ins.then_inc(o.stream.sem, o.stream.step)

        block.tensor(lambda e: replay(e, "pe"))
        block.scalar(lambda e: replay(e, "act"))
        block.vector(lambda e: replay(e, "dve"))
        block.gpsimd(lambda e: replay(e, "pool"))
        block.sync(lambda e: replay(e, "sp"))


def build_fused(NSB):
    NT = NSB * 512
    NKT = 1 + NSB * 4
    NSB2 = NSB // 4
    NT2 = NSB2 * 512
    nc = bass.Bass("TRN2", target_bir_lowering=False)
    dt_in = lambda n, s, d=F32: nc.dram_tensor(n, s, d, kind="ExternalInput").ap()
    xT2 = dt_in("xT2", [D, NT2])
    xtok = dt_in("xtok", [NT2, D])
    wg = dt_in("wg", [D, 2048])
    wba = dt_in("wba", [D, D])
    wbd = dt_in("wbd", [D, D])
    wo = dt_in("wo", [D, D])
    fnw = dt_in("fnw", [1, D])
    y = nc.dram_tensor("y", [NT2, D], F32, kind="ExternalOutput").ap()
    cinA_t = [nc.dram_tensor("cinA%d" % c, [256, NT2 // 2], F32) for c in range(4)]
    cinD_t = [nc.dram_tensor("cinD%d" % c, [256, NT2 // 2], F32) for c in range(4)]
    coutA_t = nc.dram_tensor("coutA", [4 * 1024, NT2 // 2], F32)
    coutD_t = nc.dram_tensor("coutD", [4 * 1024, NT2 // 2], F32)
    cinA = [t.ap() for t in cinA_t]
    cinD = [t.ap() for t in cinD_t]
    coutA, coutD = coutA_t.ap(), coutD_t.ap()
    GROUPS = [[0, 1, 2, 3], [4, 5, 6, 7]]
    ownA = nc.dram_tensor("ownA", [1024, NT2 // 2], F32).ap()
    ownD = nc.dram_tensor("ownD", [1024, NT2 // 2], F32).ap()
    semstack = ExitStack()
    xT = dt_in("xT", [D, NT])
    metaT = dt_in("metaT", [D, 128])
    wfm = dt_in("wfm", [D, 1792])
    wbg = dt_in("wbg", [D, 4])
    wav = dt_in("wav", [D, 256])
    normw = dt_in("normw", [128, 8])
    convw = dt_in("convw", [128, 24])
    alog = dt_in("alog", [1, 2])
    dtb = dt_in("dtb", [1, 2])
    lamv = dt_in("lamv", [1, 256])
    anw = dt_in("anw", [128, 1])
    dnw = dt_in("dnw", [128, 1])
    consts = dt_in("consts", [128, 1024])

    with ExitStack() as st:
        sb_ = lambda n, s, d=F32: st.enter_context(nc.sbuf_tensor(n, s, d))
        W_fm = sb_("W_fm", [128, 8, 2304], BF16)
        W_av = sb_("W_av", [128, 8, 256], BF16)
        KT = sb_("KT", [128, 2, NMETA + NT], BF16)
        V1 = sb_("V1", [128, 2, NKT, 129], BF16)
        x32 = sb_("x32", [128, 8, 512])
        sq = sb_("sq", [128, 2, 512], BF16)
        rstd = sb_("rstd", [128, 512])
        hnT = sb_("hnT", [128, 8, 512], BF16)
        QT = sb_("QT", [128, 2, 512], BF16)
        saz = sb_("saz", [128, 2, 512], BF16)
        sdz2 = sb_("sdz", [128, 2, 512], BF16)
        pre = sb_("pre", [128, 3, 515])
        cv = sb_("cv", [128, 3, 512])
        hist = sb_("hist", [128, 6, 3])
        beta_b = sb_("beta_b", [128, 512])
        g_b = sb_("g_b", [128, 512])
        gc_b = sb_("gc_b", [128, 512])
        egc2 = sb_("egc_b", [128, 2, 512])
        ekd_b = sb_("ekd_b", [128, 512])
        tmpf = sb_("tmpf", [128, 512])
        qnT = sb_("qnT", [128, 512], BF16)
        knT = sb_("knT", [128, 512], BF16)
        qdT2 = sb_("qdT", [128, 2, 512], BF16)
        kbgT = sb_("kbgT", [128, 512], BF16)
        kdT = sb_("kdT", [128, 512], BF16)
        vbT = sb_("vbT", [128, 512], BF16)
        PT = sb_("PT", [128, 2, 2, 512], BF16)
        ozA_s = sb_("ozA_s", [128, 2, 512], BF16)
        ozD2 = sb_("ozD_s", [128, 2, 512], BF16)
        cst = sb_("cst", [128, 1024])
        identb = sb_("identb", [128, 128], BF16)
        onesb = sb_("onesb", [128, 128], BF16)
        causb = sb_("causb", [128, 128], BF16)
        nw = sb_("nw", [128, 8])
        cw = sb_("cw", [128, 24])
        wbg32 = sb_("wbg32", [128, 8, 4])
        sm = sb_("sm", [128, 32])
        lamt = sb_("lamt", [128, 256])
        lamp = sb_("lamp", [128, 128])
        S32 = sb_("S32", [128, 2, 128])
        Sb = sb_("Sb", [128, 2, 128], BF16)
        junk = sb_("junk", [128, 128])
        junk2 = sb_("junk2", [128, 128])
        MA = sb_("MA", [128, 4, 256], BF16)
        kdm = sb_("kdm", [128, 4, 128], BF16)
        um = sb_("um", [128, 4, 128])
        wTm = sb_("wTm", [128, 4, 128], BF16)
        fz = sb_("fz", [128, 2])
        gcol = sb_("gcol", [128, 4])
        zer = sb_("zer", [128, 128])
        vnew = sb_("vnew", [128, 128], BF16)
        ot = sb_("ot", [128, 128])
        o1 = sb_("o1", [128, 4, 128])
        oo = sb_("oo", [128, 4, 128])
        junke = sb_("junke", [128, 4, 128])
        onbe = sb_("onbe", [128, 4, 128], BF16)
        sse = sb_("sse", [128, 16])
        onb = sb_("onb", [128, 128], BF16)
        ss = sb_("ss", [128, 8])
        SA = st.enter_context(nc.psum_tensor("SA", [128, 2, 512], F32))
        SB = st.enter_context(nc.psum_tensor("SB", [128, 2, 512], F32))
        ACC = [st.enter_context(nc.psum_tensor("ACC%d" % i, [128, 512], F32)) for i in range(3)]
        B = [SA[:, 0, :], SA[:, 1, :], SB[:, 0, :], ACC[0][:], ACC[1][:], ACC[2][:], SB[:, 1, :]]
        SBANK = [(SA, (0, 1)), (SB, (2, 6))]
        pTr = st.enter_context(nc.psum_tensor("pTr", [128, 1024], BF16))

        identf = cst[:, 0:128]
        inclT = cst[:, 128:256]
        strictT = cst[:, 256:384]
        resetm = cst[:, 512:1024]
        P = Prog(nc, "p1")
        sink = [None]

        def op(issuer, emit, reads=(), writes=(), stream=None, after=()):
            if sink[0] is not None:
                sink[0].append((issuer, emit, tuple(reads), tuple(writes)))
                return None
            return P.op(issuer, emit, reads=reads, writes=writes, stream=stream, after=after)

        def emit_list(lst, extra_reads=()):
            for it_ in lst:
                if it_[0] == "dma":
                    _, q_, o_, i_, r_, w_ = it_
                    dma(o_, i_, reads=tuple(r_) + tuple(extra_reads), writes=w_, q=q_)
                else:
                    (i_, e_, r_, w_) = it_
                    P.op(i_, e_, reads=tuple(r_) + tuple(extra_reads), writes=w_)

        TX = "x32tok"

        def fence():
            op("pool", lambda e: e.memset(fz[:, 0:1], 0.0), writes=[TX])

        def interleave(a_, b_):
            out_, ia, ib = [], 0, 0
            while ia < len(a_) or ib < len(b_):
                if ib >= len(b_) or (ia < len(a_) and ia * len(b_) <= ib * len(a_)):
                    out_.append(a_[ia]); ia += 1
                else:
                    out_.append(b_[ib]); ib += 1
            return out_
        dq = [0]

        dlast = {}

        def dma(out, in_, reads=(), writes=(), q=None):
            q = q or "sp"
            if sink[0] is not None:
                sink[0].append(("dma", q, out, in_, tuple(reads), tuple(writes)))
                return None
            dq[0] += 1
            name = "%s:%d" % (q, dq[0] % 8)
            prev = dlast.get(name)
            o_ = op(q, lambda e, o=out, i=in_: e.dma_start(out=o, in_=i), reads=reads, writes=writes, stream=name, after=([prev] if prev is not None else ()))
            dlast[name] = o_
            return o_

        def mm(out, lhsT, rhs, start=True, stop=True, reads=(), writes=()):
            return op("pe", lambda e, o=out, l=lhsT, r=rhs, s0=start, s1=stop: e.matmul(o, lhsT=l, rhs=r, start=s0, stop=s1, skip_group_check=True), reads=reads, writes=writes)

        def tr(out, in_, reads=(), writes=()):
            return op("pe", lambda e, o=out, i=in_: e.transpose(o, i, identb[:]), reads=list(reads) + ["identb"], writes=writes)

        def act(out, in_, func, scale=1.0, bias=None, accum=None, reads=(), writes=()):
            def f(e, o=out, i=in_, fn=func, s=scale, b=bias, a=accum):
                kw = {}
                if b is not None:
                    kw["bias"] = b
                if a is not None:
                    kw["accum_out"] = a
                return e.activation(out=o, in_=i, func=fn, scale=s, **kw)
            return op("act", f, reads=reads, writes=writes)

        def ve(eng, fn, reads=(), writes=()):
            return op(eng, fn, reads=reads, writes=writes)

        def tt(eng, out, a, b, alu, reads=(), writes=()):
            return op(eng, lambda e, o=out, x=a, y=b, u=alu: e.tensor_tensor(out=o, in0=x, in1=y, op=u), reads=reads, writes=writes)

        def ts(eng, out, a, s1, s2, op0, op1=None, reads=(), writes=()):
            def f(e, o=out, x=a, p=s1, q=s2, u=op0, v=op1):
                if v is None:
                    return e.tensor_scalar(out=o, in0=x, scalar1=p, scalar2=None, op0=u)
                return e.tensor_scalar(out=o, in0=x, scalar1=p, scalar2=q, op0=u, op1=v)
            return op(eng, f, reads=reads, writes=writes)

        def stt(eng, out, a, s, b, op0, op1, reads=(), writes=()):
            return op(eng, lambda e, o=out, x=a, p=s, y=b, u=op0, v=op1: e.scalar_tensor_tensor(out=o, in0=x, scalar=p, in1=y, op0=u, op1=v), reads=reads, writes=writes)

        dma(cst[:], consts[:, :], writes=["cst"])
        dma(nw[:], normw[:, :], writes=["nw"])
        dma(cw[:], convw[:, :], writes=["cw"])
        dma(wbg32[:], wbg.rearrange("(k p) c -> p k c", p=128), writes=["wbg32"])
        dma(sm[:, 4:6], alog.partition_broadcast(128), writes=["sm_a"])
        dma(sm[:, 6:8], dtb.partition_broadcast(128), writes=["sm_d"])
        dma(sm[:, 8:9], anw[:, :], writes=["sm_anw"])
        dma(sm[:, 9:10], dnw[:, :], writes=["sm_dnw"])
        dma(lamt[:], lamv.partition_broadcast(128), writes=["lamt"])
        ve("dve", lambda e: e.tensor_copy(out=identb[:], in_=identf), reads=["cst"], writes=["identb"])
        ve("dve", lambda e: e.tensor_copy(out=causb[:], in_=cst[:, 384:512]), reads=["cst"], writes=["causb"])
        ve("dve", lambda e: e.memset(onesb[:], 1.0), writes=["onesb"])
        ve("pool", lambda e: e.memset(V1[:, :, :, 128:129], 1.0), writes=["V1"])
        ve("pool", lambda e: e.memset(hist[:], 0.0), writes=["hist"])
        ve("pool", lambda e: e.memset(S32[:], 0.0), writes=["S32_0", "S32_1"])
        ve("pool", lambda e: e.memset(Sb[:], 0.0), writes=["Sb_0", "Sb_1"])
        ve("pool", lambda e: e.memset(zer[:], 0.0), writes=["zer"])
        ve("pool", lambda e: e.memset(vnew[:], 0.0), writes=["vnew"])
        ve("pool", lambda e: e.memset(ot[:], 0.0), writes=["ot"])
        tt("dve", lamp[:, 0:64], lamt[:, 0:64], lamt[:, 64:128], ALU.mult, reads=["lamt"], writes=["lamp"])
        tt("dve", lamp[:, 64:128], lamt[:, 128:192], lamt[:, 192:256], ALU.mult, reads=["lamt"], writes=["lamp"])
        ve("dve", lambda e: e.reduce_sum(out=sm[:, 2:3], in_=lamp[:, 0:64], axis=AX.X), reads=["lamp"], writes=["sm_s"])
        ve("dve", lambda e: e.reduce_sum(out=sm[:, 3:4], in_=lamp[:, 64:128], axis=AX.X), reads=["lamp"], writes=["sm_s"])
        act(sm[:, 2:4], sm[:, 2:4], AF.Exp, reads=["sm_s"], writes=["sm_s"])
        tt("dve", sm[:, 0:1], sm[:, 2:3], sm[:, 3:4], ALU.subtract, reads=["sm_s"], writes=["sm_lam"])
        ts("dve", sm[:, 1:2], sm[:, 0:1], -1.0, -LAM_INIT, ALU.mult, ALU.add, reads=["sm_lam"], writes=["sm_lam"])
        ts("dve", sm[:, 8:9], sm[:, 8:9], 1.0 - LAM_INIT, None, ALU.mult, reads=["sm_anw"], writes=["sm_anw"])
        act(sm[:, 4:6], sm[:, 4:6], AF.Exp, reads=["sm_a"], writes=["sm_a"])
        ts("dve", sm[:, 4:6], sm[:, 4:6], -1.0, None, ALU.mult, reads=["sm_a"], writes=["sm_a"])
        x32f = x32[:].rearrange("p a b -> p (a b)")
        for kc in range(8):
            hb = (kc % 2) * 2048
            sk = [("x", 4 * (kc % 2) + i) for i in range(4)]
            dma(x32f[:, hb:hb + 1792], wfm[kc * 128:(kc + 1) * 128, :], writes=sk)
            dma(x32f[:, hb + 1792:hb + 2048], wav[kc * 128:(kc + 1) * 128, :], reads=sk, writes=[("wavd", kc)], q="pool")
            ts("dve", W_fm[:, kc, 0:1792], x32f[:, hb:hb + 1792], nw[:, kc:kc + 1], None, ALU.mult, reads=sk + ["nw"], writes=["W_fm"])
            act(W_av[:, kc, :], x32f[:, hb + 1792:hb + 2048], AF.Copy, scale=nw[:, kc:kc + 1], reads=sk + ["nw", ("wavd", kc)], writes=["W_av"])
            for j in range(4):
                ts("dve", W_fm[:, kc, 1792 + j * 128:1920 + j * 128], wbg32[:, kc, j:j + 1].to_broadcast([128, 128]), nw[:, kc:kc + 1], None, ALU.mult, reads=["wbg32", "nw"], writes=["W_fm"])
        bank = [0]

        ipb = [(0, 1)]

        def nextbank():
            bank[0] ^= 1
            return ipb[0][bank[0]]

        def inproj_fm(c, T):
            b = nextbank()
            for kc in range(8):
                mm(B[b][:, 0:T], W_fm[:, kc, c * 128:(c + 1) * 128], hnT[:, kc, 0:T], start=(kc == 0), stop=(kc == 7), reads=["W_fm", "hnT"], writes=[("B", b)])
            return b

        def superblock(sbi, prevRC):
            meta = sbi < 0
            T = 128 if meta else 512
            src = metaT if meta else xT
            c0 = 0 if meta else sbi * 512
            sink[0] = []
            fence()
            for kc in range(8):
                dma(x32[:, kc, 0:T], src[kc * 128:(kc + 1) * 128, c0:c0 + T], reads=[TX], writes=[("x", kc)], q=("sp" if kc % 2 == 0 else "pool"))
            for kc in range(8):
                act(sq[:, kc % 2, 0:T], x32[:, kc, 0:T], AF.Square, reads=[("x", kc), TX], writes=[("sq", kc % 2)])
                mm(B[2][:, 0:T], onesb[:], sq[:, kc % 2, 0:T], start=(kc == 0), stop=(kc == 7), reads=["onesb", ("sq", kc % 2)], writes=[("B", 2)])
            ts("dve", rstd[:, 0:T], B[2][:, 0:T], 1.0 / D, EPS, ALU.mult, ALU.add, reads=[("B", 2)], writes=["rstd"])
            act(rstd[:, 0:T], rstd[:, 0:T], AF.Ln, reads=["rstd"], writes=["rstd"]); act(rstd[:, 0:T], rstd[:, 0:T], AF.Exp, scale=-0.5, reads=["rstd"], writes=["rstd"])
            for kc in range(8):
                tt("dve" if kc % 2 == 0 else "pool", hnT[:, kc, 0:T], x32[:, kc, 0:T], rstd[:, 0:T], ALU.mult, reads=[("x", kc), "rstd", TX], writes=["hnT"])
            for h in range(2):
                koff = 0 if meta else NMETA + sbi * 512
                if not meta:
                    b = inproj_fm(0 + h, T)
                    act(QT[:, h, 0:T], B[b][:, 0:T], AF.Copy, reads=[("B", b)], writes=[("QT", h)])
                b = inproj_fm(2 + h, T)
                if meta:
                    ve("dve", lambda e, b=b, h=h: e.tensor_copy(out=KT[:, h, 0:16], in_=B[b][:, 112:128]), reads=[("B", b)], writes=[("KT", h, -1)])
                else:
                    ve("dve", lambda e, b=b, h=h, k=koff: e.tensor_copy(out=KT[:, h, k:k + 512], in_=B[b][:, 0:512]), reads=[("B", b)], writes=[("KT", h, sbi)])
                if not meta:
                    b = inproj_fm(4 + h, T)
                    act(saz[:, h, 0:T], B[b][:, 0:T], AF.Silu, reads=[("B", b)], writes=[("saz", h)])
            ntile = 1 if meta else 4
            for j in range(ntile):
                b = nextbank()
                for kc in range(8):
                    lh = hnT[:, kc, 112:128] if meta else hnT[:, kc, j * 128:(j + 1) * 128]
                    np_ = 16 if meta else 128
                    mm(B[b][0:np_, 0:256], lh, W_av[:, kc, :], start=(kc == 0), stop=(kc == 7), reads=["hnT", "W_av"], writes=[("B", b)])
                kt = 0 if meta else 1 + sbi * 4 + j
                for h in range(2):
                    np_ = 16 if meta else 128
                    ve("dve", lambda e, b=b, h=h, kt=kt, n=np_: e.tensor_copy(out=V1[0:n, h, kt, 0:128], in_=B[b][0:n, h * 128:(h + 1) * 128]), reads=[("B", b)], writes=[("V1", h, kt)])
            AB = sink[0]
            sink[0] = None
            emit_list(interleave(AB, prevRC))
            if sbi > 0 and sbi % NSB2 == 0:
                collect(sbi // NSB2 - 1)
            EPI1 = []
            if not meta:
                for h in range(2):
                    nkt = 1 + 4 * sbi + 4

                    def att_qk(kt):
                        nk = 16 if kt == 0 else 128
                        ks = 0 if kt == 0 else NMETA + (kt - 1) * 128
                        a = kt - 1 - 4 * sbi
                        diag = a >= 0
                        qlo = a * 128 if diag else 0
                        buf = kt % 2
                        ksb = -1 if kt == 0 else (kt - 1) // 4
                        Sps, (k0, k1) = SBANK[buf]
                        for m in range(2):
                            mm(Sps[0:nk, m, qlo:512], KT[m * 64:(m + 1) * 64, h, ks:ks + nk], QT[m * 64:(m + 1) * 64, h, qlo:512], reads=[("KT", h, ksb), ("QT", h)], writes=[("B", (k0, k1)[m])])
                        act(PT[0:nk, buf, :, qlo:512], Sps[0:nk, :, qlo:512], AF.Exp, scale=0.125, reads=[("B", k0), ("B", k1)], writes=[("PT", buf, 0), ("PT", buf, 1)])
                        for m in range(2):
                            if diag:
                                tt("pool", PT[:, buf, m, qlo:qlo + 128], PT[:, buf, m, qlo:qlo + 128], causb[:], ALU.mult, reads=[("PT", buf, m), "causb"], writes=[("PT", buf, m)])

                    def att_pv(kt):
                        nk = 16 if kt == 0 else 128
                        a = kt - 1 - 4 * sbi
                        buf = kt % 2
                        for qi in range(max(a, 0), 4):
                            for m in range(2):
                                last = (kt == 4 * sbi + qi + 1)
                                r_ = qi * 2 + m
                                bk, off = 3 + r_ // 3, (r_ % 3) * 129
                                mm(B[bk][:, off:off + 129], PT[0:nk, buf, m, qi * 128:(qi + 1) * 128], V1[0:nk, h, kt, 0:129], start=(kt == 0 and r_ % 3 == 0), stop=last, reads=[("PT", buf, m), ("V1", h, kt)], writes=[("B", bk)])

                    for step in range(nkt + 1):
                        if step < nkt:
                            att_qk(step)
                        if step >= 1:
                            att_pv(step - 1)
                    qchains = []
                    for qi in range(4):
                        sink[0] = []
                        b0_, f0_ = 3 + (qi * 2) // 3, ((qi * 2) % 3) * 129
                        b1_, f1_ = 3 + (qi * 2 + 1) // 3, ((qi * 2 + 1) % 3) * 129
                        a0_, a1_ = B[b0_][:, f0_:f0_ + 129], B[b1_][:, f1_:f1_ + 129]
                        kk = ("B", b0_)
                        kk1 = ("B", b1_)
                        q4 = qi * 4
                        kq = lambda n, qi=qi: (n, qi)
                        o1q, ooq, jq, onq = o1[:, qi, :], oo[:, qi, :], junke[:, qi, :], onbe[:, qi, :]
                        rgn = slice(qi * 256, qi * 256 + 128)
                        kr = "pTr%d" % (2 * qi)
                        ve("dve", lambda e, a0_=a0_, q4=q4: e.reciprocal(out=sse[:, q4:q4 + 1], in_=a0_[:, 128:129]), reads=[kk], writes=[kq("s0")])
                        ve("dve", lambda e, a1_=a1_, q4=q4: e.reciprocal(out=sse[:, q4 + 1:q4 + 2], in_=a1_[:, 128:129]), reads=[kk1], writes=[kq("s1")])
                        tt("dve", sse[:, q4 + 2:q4 + 3], sse[:, q4 + 1:q4 + 2], sm[:, 1:2], ALU.mult, reads=[kq("s1"), "sm_lam"], writes=[kq("s2")])
                        ts("dve", o1q, a0_[:, 0:128], sse[:, q4:q4 + 1], None, ALU.mult, reads=[kk, kq("s0")], writes=[kq("o1")])
                        stt("dve", ooq, a1_[:, 0:128], sse[:, q4 + 2:q4 + 3], o1q, ALU.mult, ALU.add, reads=[kk1, kq("s2"), kq("o1")], writes=[kq("oo")])
                        ve("pool", lambda e, q4=q4: e.memset(sse[:, q4 + 3:q4 + 4], 0.0), writes=[kq("s3")])
                        act(jq, ooq, AF.Square, accum=sse[:, q4 + 3:q4 + 4], reads=[kq("oo")], writes=[kq("junke"), kq("s3")])
                        ts("dve", sse[:, q4 + 3:q4 + 4], sse[:, q4 + 3:q4 + 4], 1.0 / 128, EPS, ALU.mult, ALU.add, reads=[kq("s3")], writes=[kq("s3")])
                        act(sse[:, q4 + 3:q4 + 4], sse[:, q4 + 3:q4 + 4], AF.Ln, reads=[kq("s3")], writes=[kq("s3")])
                        act(sse[:, q4 + 3:q4 + 4], sse[:, q4 + 3:q4 + 4], AF.Exp, scale=-0.5, reads=[kq("s3")], writes=[kq("s3")])
                        ts("dve", onq, ooq, sse[:, q4 + 3:q4 + 4], None, ALU.mult, reads=[kq("oo"), kq("s3")], writes=[kq("onbe")])
                        tr(pTr[:, rgn], onq, reads=[kq("onbe")], writes=[kr, "pTrbank"])
                        stt("dve", ozA_s[:, h, qi * 128:(qi + 1) * 128], pTr[:, rgn], sm[:, 8:9], saz[:, h, qi * 128:(qi + 1) * 128], ALU.mult, ALU.mult, reads=["pTrbank", kr, "sm_anw", ("saz", h)], writes=[("ozA_s", h, qi)])
                        qchains.append(sink[0])
                    epl = []
                    for l_ in qchains:
                        epl = interleave(epl, l_)
                    sink[0] = [] if h == 1 else None
                    if h == 0:
                        emit_list(epl)
                    else:
                        sink[0] = list(epl)
                    dma(cinA[sbi // NSB2][h * 128:(h + 1) * 128, (sbi % NSB2) * 256:(sbi % NSB2) * 256 + 256], ozA_s[:, h, :].bitcast(F32), reads=[("ozA_s", h, q_) for q_ in range(4)], writes=[("cinA", h, sbi)])
                    if h == 1:
                        EPI1 = sink[0]
                        sink[0] = None
            Wl, PRl, RCl = [], [], []
            for h in range(2):
                sdz, egc_b, qdT, ozD_s = sdz2[:, h, :], egc2[:, h, :], qdT2[:, h, :], ozD2[:, h, :]
                kE, kQ, kZ, kO = ("egc_b", h), ("qdT", h), ("sdz", h), ("ozD_s", h)
                sink[0] = []
                if h == 1:
                    ipb[0] = (3, 4)
                for j in range(3):
                    b = inproj_fm(6 + 2 * j + h, T)
                    ve("pool", lambda e, j=j, h=h: e.tensor_copy(out=pre[:, j, 0:3], in_=hist[:, h * 3 + j, :]), reads=["hist"], writes=[("pre", j)])
                    act(pre[:, j, 3:3 + T], B[b][:, 0:T], AF.Copy, reads=[("B", b)], writes=[("pre", j)])
                if not meta:
                    b = inproj_fm(12 + h, T)
                    act(sdz[:, 0:T], B[b][:, 0:T], AF.Silu, reads=[("B", b)], writes=[kZ])
                b = inproj_fm(16 + h, T)
                act(g_b[:, 0:T], B[b][:, 0:T], AF.Exp, bias=sm[:, 6 + h:7 + h], reads=[("B", b), "sm_d"], writes=["g_b"])
                act(g_b[:, 0:T], g_b[:, 0:T], AF.Ln, bias=1.0, reads=["g_b"], writes=["g_b"])
                ipb[0] = (0, 1)
                wsplit = len(sink[0])
                b = inproj_fm(14 + h, T)
                act(beta_b[:, 0:T], B[b][:, 0:T], AF.Sigmoid, reads=[("B", b)], writes=["beta_b"])
                for j in range(3):
                    w = (2 * j + h) * 4
                    ts("dve", cv[:, j, 0:T], pre[:, j, 0:T], cw[:, w:w + 1], None, ALU.mult, reads=[("pre", j), "cw"], writes=[("cv", j)])
                    for tap in range(1, 4):
                        stt("dve", cv[:, j, 0:T], pre[:, j, tap:tap + T], cw[:, w + tap:w + tap + 1], cv[:, j, 0:T], ALU.mult, ALU.add, reads=[("pre", j), "cw", ("cv", j)], writes=[("cv", j)])
                    ve("pool", lambda e, j=j, h=h, T=T: e.tensor_copy(out=hist[:, h * 3 + j, :], in_=pre[:, j, T:T + 3]), reads=[("pre", j)], writes=["hist"])
                ts("dve", g_b[:, 0:T], g_b[:, 0:T], sm[:, 4 + h:5 + h], None, ALU.mult, reads=["g_b", "sm_a"], writes=["g_b"])
                ve("dve", lambda e, T=T: e.tensor_tensor_scan(out=gc_b[:, 0:T], data0=resetm[:, 0:T], data1=g_b[:, 0:T], initial=0.0, op0=ALU.mult, op1=ALU.add), reads=["cst", "g_b"], writes=["gc_b"])
                for j in range(3):
                    act(cv[:, j, 0:T], cv[:, j, 0:T], AF.Silu, reads=[("cv", j)], writes=[("cv", j)])
                act(egc_b[:, 0:T], gc_b[:, 0:T], AF.Exp, reads=["gc_b"], writes=[kE])
                scr = (tmpf, ekd_b)
                bnk = (2, 6)
                for j in range(2):
                    act(sq[:, j, 0:T], cv[:, j, 0:T], AF.Square, reads=[("cv", j)], writes=[("sq", j)])
                for j in range(2):
                    mm(B[bnk[j]][:, 0:T], onesb[:], sq[:, j, 0:T], reads=["onesb", ("sq", j)], writes=[("B", bnk[j])])
                for j in range(2):
                    act(scr[j][:, 0:T], B[bnk[j]][:, 0:T], AF.Ln, bias=1e-6, reads=[("B", bnk[j])], writes=[("scr", j)])
                for j in range(2):
                    act(scr[j][:, 0:T], scr[j][:, 0:T], AF.Exp, scale=-0.5, reads=[("scr", j)], writes=[("scr", j)])
                for j, dst in ((0, qnT), (1, knT)):
                    stt("dve", dst[:, 0:T], cv[:, j, 0:T], (128.0 ** -0.5) if j == 0 else 1.0, scr[j][:, 0:T], ALU.mult, ALU.mult, reads=[("cv", j), ("scr", j)], writes=["qnT" if j == 0 else "knT"])
                tt("dve", vbT[:, 0:T], cv[:, 2, 0:T], beta_b[:, 0:T], ALU.mult, reads=[("cv", 2), "beta_b"], writes=["vbT"])
                tt("dve", qdT[:, 0:T], qnT[:, 0:T], egc_b[:, 0:T], ALU.mult, reads=["qnT", kE], writes=[kQ])
                tt("pool", tmpf[:, 0:T], beta_b[:, 0:T], egc_b[:, 0:T], ALU.mult, reads=["beta_b", kE, ("scr", 0)], writes=[("scr", 0)])
                tt("pool", kbgT[:, 0:T], knT[:, 0:T], tmpf[:, 0:T], ALU.mult, reads=["knT", ("scr", 0)], writes=["kbgT"])
                nch = T // 64
                for ch in range(nch):
                    ts("dve", ekd_b[:, ch * 64:(ch + 1) * 64], gc_b[:, ch * 64:(ch + 1) * 64], gc_b[:, ch * 64 + 63:ch * 64 + 64], None, ALU.subtract, reads=["gc_b", ("scr", 1)], writes=[("scr", 1)])
                act(ekd_b[:, 0:T], ekd_b[:, 0:T], AF.Exp, scale=-1.0, reads=[("scr", 1)], writes=[("scr", 1)])
                tt("pool", kdT[:, 0:T], knT[:, 0:T], ekd_b[:, 0:T], ALU.mult, reads=["knT", ("scr", 1)], writes=["kdT"])
                Wl.append(sink[0])
                sink[0] = None
                Sk, Sbk = "S32_%d" % h, "Sb_%d" % h

                def prep(p):
                    pc = slice(p * 128, (p + 1) * 128)
                    s_ = p % 4
                    bi_ = (0, 1, 2, 6)[s_]
                    Bp, bk = B[bi_], ("B", bi_)
                    xb_ = s_ * 1024
                    Eb_s, EE_s, junkp_s = x32f[:, xb_:xb_ + 128], x32f[:, xb_ + 128:xb_ + 384], x32f[:, xb_ + 384:xb_ + 512]
                    bfv = x32f[:, xb_ + 512:xb_ + 1024].bitcast(BF16)
                    PPv = lambda c_: bfv[:, c_ * 256:(c_ + 1) * 256]
                    Ybv = lambda c_: bfv[:, 512 + c_ * 128:512 + (c_ + 1) * 128]
                    kbgm_s, vbm_s = bfv[:, 768:896], bfv[:, 896:1024]
                    k = lambda n: (n, s_)
                    ra, rb = slice(2 * s_ * 128, (2 * s_ + 1) * 128), slice((2 * s_ + 1) * 128, (2 * s_ + 2) * 128)
                    ka, kb = "pTr%d" % (2 * s_), "pTr%d" % (2 * s_ + 1)
                    Es = Eb_s
                    MTs, ATs = MA[:, s_, 0:128], MA[:, s_, 128:256]
                    tt("dve", junkp_s, gc_b[:, pc], identf, ALU.mult, reads=["gc_b", "cst"], writes=[k("junkp")])
                    ve("dve", lambda e, s_=s_, j_=junkp_s: e.reduce_sum(out=gcol[:, s_:s_ + 1], in_=j_, axis=AX.X), reads=[k("junkp")], writes=[k("gcol")])
                    stt("dve", Es, gc_b[:, pc], gcol[:, s_:s_ + 1], zer[:], ALU.subtract, ALU.min, reads=["gc_b", k("gcol"), "zer"], writes=[k("E")])
                    act(Es, Es, AF.Exp, reads=[k("E")], writes=[k("E")])
                    tt("pool", EE_s[:, 128:256], Es, inclT, ALU.mult, reads=[k("E"), "cst"], writes=[k("EI")])
                    tt("pool", EE_s[:, 0:128], Es, strictT, ALU.mult, reads=[k("E"), "cst"], writes=[k("EB")])
                    tt("pool", EE_s[:, 0:128], EE_s[:, 0:128], beta_b[:, pc], ALU.mult, reads=[k("EB"), "beta_b"], writes=[k("EB")])
                    mm(Bp[:, 0:128], knT[:, pc], knT[:, pc], reads=["knT"], writes=[bk])
                    mm(Bp[:, 128:256], knT[:, pc], qnT[:, pc], reads=["knT", "qnT"], writes=[bk])
                    tt("dve", MA[:, s_, :], Bp[:, 0:256], EE_s, ALU.mult, reads=[bk, k("EB"), k("EI")], writes=[k("MA")])
                    tr(pTr[:, ra], MTs, reads=[k("MA")], writes=[ka, "pTrbank"])
                    act(PPv(0)[:, 0:128], pTr[:, ra], AF.Copy, reads=["pTrbank", ka], writes=[("PP", s_, 0)])
                    tt("dve", Ybv(0), identf, MTs, ALU.subtract, reads=["cst", k("MA")], writes=[("Yb", s_, 0)])
                    cur = 0
                    for lvl in range(1, 6):
                        nx = cur ^ 1
                        Pc = PPv(cur)[:, 0:128]
                        PTc = MTs if lvl == 1 else PPv(cur)[:, 128:256]
                        rk = [("PP", s_, cur), k("MA")]
                        mm(Bp[:, 0:128], PTc, Pc, reads=rk, writes=[bk])
                        if lvl < 5:
                            mm(Bp[:, 128:256], Pc, PTc, reads=rk, writes=[bk])
                            act(PPv(nx), Bp[:, 0:256], AF.Copy, reads=[bk], writes=[("PP", s_, nx)])
                        else:
                            act(PPv(nx)[:, 0:128], Bp[:, 0:128], AF.Copy, reads=[bk], writes=[("PP", s_, nx)])
                        mm(Bp[:, 256:384], PPv(nx)[:, 0:128], Ybv(cur), reads=[("PP", s_, nx), ("Yb", s_, cur)], writes=[bk])
                        tt("dve", Ybv(nx), Bp[:, 256:384], Ybv(cur), ALU.add, reads=[bk, ("Yb", s_, cur)], writes=[("Yb", s_, nx)])
                        cur = nx
                    assert cur == 1
                    Y, Yk = Ybv(1), ("Yb", s_, 1)
                    tr(pTr[:, rb], kbgT[:, pc], reads=["kbgT"], writes=[kb, "pTrbank"])
                    act(kbgm_s, pTr[:, rb], AF.Copy, reads=["pTrbank", kb], writes=[k("kbgm")])
                    tr(pTr[:, rb], vbT[:, pc], reads=["vbT"], writes=[kb, "pTrbank"])
                    ve("dve", lambda e, v_=vbm_s, rb=rb: e.tensor_copy(out=v_, in_=pTr[:, rb]), reads=["pTrbank", kb], writes=[k("vbm")])
                    tr(pTr[:, ra], kdT[:, pc], reads=["kdT"], writes=[ka, "pTrbank"])
                    act(kdm[:, s_, :], pTr[:, ra], AF.Copy, reads=["pTrbank", ka], writes=[k("kdm")])
                    mm(Bp[:, 384:512], Y, vbm_s, reads=[Yk, k("vbm")], writes=[bk])
                    act(um[:, s_, :], Bp[:, 384:512], AF.Copy, reads=[bk], writes=[k("um")])
                    mm(Bp[:, 256:384], kbgm_s, Y, reads=[Yk, k("kbgm")], writes=[bk])
                    ve("dve", lambda e, s_=s_, Bp=Bp: e.tensor_copy(out=wTm[:, s_, :], in_=Bp[:, 256:384]), reads=[bk], writes=[k("wTm")])

                def rec(p):
                    pc = slice(p * 128, (p + 1) * 128)
                    s_ = p % 4
                    ra = slice(2 * s_ * 128, (2 * s_ + 1) * 128)
                    ka = "pTr%d" % (2 * s_)
                    k = lambda n: (n, s_)
                    for c in ((1,) if meta else (0, 1)):
                        hs = slice(c * 64, (c + 1) * 64)
                        mm(B[3][:, 0:128], wTm[:, s_, :], Sb[:, h, :], reads=[k("wTm"), Sbk], writes=[("B", 3)])
                        tt("dve", vnew[hs, :], um[hs, s_, :], B[3][hs, 0:128], ALU.subtract, reads=[k("um"), ("B", 3)], writes=["vnew"])
                        mm(B[4][:, 0:128], qdT[:, pc], Sb[:, h, :], start=True, stop=False, reads=[kQ, Sbk], writes=[("B", 4)])
                        mm(B[4][:, 0:128], MA[:, s_, 128:256], vnew[:], start=False, stop=True, reads=[k("MA"), "vnew"], writes=[("B", 4)])
                        act(ot[hs, :], B[4][hs, 0:128], AF.Copy, reads=[("B", 4)], writes=["ot"])
                        mm(B[5][:, 0:128], kdm[hs, s_, :], vnew[hs, :], reads=[k("kdm"), "vnew"], writes=[("B", 5)])
                        cdc = p * 128 + c * 64 + 63
                        stt("dve", Sb[:, h, :], S32[:, h, :], egc_b[:, cdc:cdc + 1], B[5][:, 0:128], ALU.mult, ALU.add, reads=[Sk, kE, ("B", 5)], writes=[Sbk])
                        stt("dve", S32[:, h, :], S32[:, h, :], egc_b[:, cdc:cdc + 1], B[5][:, 0:128], ALU.mult, ALU.add, reads=[Sk, kE, ("B", 5)], writes=[Sk])
                    if not meta:
                        ve("pool", lambda e: e.memset(ss[:, 5:6], 0.0), writes=["ss5"])
                        act(junk2[:], ot[:], AF.Square, accum=ss[:, 5:6], reads=["ot"], writes=["junk2", "ss5"])
                        ts("dve", ss[:, 5:6], ss[:, 5:6], 1.0 / 128, EPS, ALU.mult, ALU.add, reads=["ss5"], writes=["ss5"])
                        act(ss[:, 5:6], ss[:, 5:6], AF.Ln, reads=["ss5"], writes=["ss5"]); act(ss[:, 5:6], ss[:, 5:6], AF.Exp, scale=-0.5, reads=["ss5"], writes=["ss5"])
                        ts("dve", onb[:], ot[:], ss[:, 5:6], None, ALU.mult, reads=["ot", "ss5"], writes=["onb"])
                        tr(pTr[:, ra], onb[:], reads=["onb"], writes=[ka, "pTrbank"])
                        stt("dve", ozD_s[:, pc], pTr[:, ra], sm[:, 9:10], sdz[:, pc], ALU.mult, ALU.mult, reads=["pTrbank", ka, "sm_dnw", kZ], writes=[kO])

                npairs = T // 128
                preps, recs = [], []
                for p in range(npairs):
                    sink[0] = []
                    prep(p)
                    preps.append(sink[0])
                    sink[0] = []
                    rec(p)
                    recs.append(sink[0])
                sink[0] = None
                lst = []
                for l_ in preps:
                    lst = interleave(lst, l_)
                PRl.append(lst)
                rc_ = []
                for p in range(npairs):
                    rc_ = rc_ + recs[p]
                if not meta:
                    sink[0] = []
                    dma(cinD[sbi // NSB2][h * 128:(h + 1) * 128, (sbi % NSB2) * 256:(sbi % NSB2) * 256 + 256], ozD_s.bitcast(F32), reads=[kO], writes=[("cinD", h, sbi)])
                    rc_ = rc_ + sink[0]
                    sink[0] = None
                RCl.append(rc_)
            emit_list(interleave(EPI1, Wl[0]))
            fence()
            emit_list(interleave(PRl[0], Wl[1][:wsplit]), extra_reads=(TX,))
            emit_list(interleave(RCl[0], Wl[1][wsplit:]))
            fence()
            emit_list(PRl[1], extra_reads=(TX,))
            return RCl[1]

        def collect(c):
            kA = [("cinA", h, sj) for h in range(2) for sj in range(c * NSB2, (c + 1) * NSB2)]
            kD = [("cinD", h, sj) for h in range(2) for sj in range(c * NSB2, (c + 1) * NSB2)]
            op("pool", lambda e, c=c: e.collective_compute("AllGather", ALU.bypass, replica_groups=GROUPS, ins=[cinA_t[c].ap().opt()], outs=[coutA[c * 1024:(c + 1) * 1024, :].opt()]), reads=kA, writes=[("coutA", c)], stream="ccA")
            op("pool", lambda e, c=c: e.collective_compute("AllGather", ALU.bypass, replica_groups=GROUPS, ins=[cinD_t[c].ap().opt()], outs=[coutD[c * 1024:(c + 1) * 1024, :].opt()]), reads=kD, writes=[("coutD", c)], stream="ccD")

        pend = superblock(-1, [])
        for sbi in range(NSB):
            pend = superblock(sbi, pend)
        emit_list(pend)
        collect(NSB // NSB2 - 1)
        P.barrier()
        P.finalize(st, semstack)
        P1 = P

    NT = NT2


    with ExitStack() as st:
        sb_ = lambda n, s, d=F32: st.enter_context(nc.sbuf_tensor("q_" + n, s, d))
        Wg = sb_("Wg", [128, 8, 2048], BF16)
        Wba = sb_("Wba", [128, 8, D], BF16)
        Wbd = sb_("Wbd", [128, 8, D], BF16)
        Wo = sb_("Wo", [128, 8, D], BF16)
        x32 = sb_("x32", [128, 8, 512])
        sq = sb_("sq", [128, 2, 512], BF16)
        rstd = sb_("rstd", [128, 512])
        hnT = sb_("hnT", [128, 8, 512], BF16)
        oA = sb_("oA", [128, 8, 512], BF16)
        oD = sb_("oD", [128, 8, 512], BF16)
        sg = sb_("sg", [128, 16, 512])
        tA = sb_("tA", [128, 512])
        mT = sb_("mT", [128, 8, 512], BF16)
        xt = sb_("xt", [128, D])
        h2 = sb_("h2", [128, D])
        yo = sb_("yo", [128, D])
        junk = sb_("junk", [128, D])
        fw = sb_("fw", [128, D])
        nw = sb_("nw", [128, 8])
        onesb = sb_("onesb", [128, 128], BF16)
        ss = sb_("ss", [128, 4])
        B = [st.enter_context(nc.psum_tensor("QB%d" % i, [128, 512], F32)) for i in range(6)]
        P = Prog(nc, "p2")
        op = P.op
        dq = [0]

        dlast = {}

        def dma(out, in_, reads=(), writes=(), q=None):
            q = q or "sp"
            dq[0] += 1
            name = "%s:%d" % (q, dq[0] % 8)
            prev = dlast.get(name)
            o_ = op(q, lambda e, o=out, i=in_: e.dma_start(out=o, in_=i), reads=reads, writes=writes, stream=name, after=([prev] if prev is not None else ()))
            dlast[name] = o_
            return o_

        def mm(out, lhsT, rhs, start=True, stop=True, reads=(), writes=()):
            return op("pe", lambda e, o=out, l=lhsT, r=rhs, s0=start, s1=stop: e.matmul(o, lhsT=l, rhs=r, start=s0, stop=s1, skip_group_check=True), reads=reads, writes=writes)

        def act(out, in_, func, scale=1.0, accum=None, reads=(), writes=()):
            def f(e, o=out, i=in_, fn=func, s=scale, a=accum):
                kw = {}
                if a is not None:
                    kw["accum_out"] = a
                return e.activation(out=o, in_=i, func=fn, scale=s, **kw)
            return op("act", f, reads=reads, writes=writes)

        def tt(eng, out, a, b, alu, reads=(), writes=()):
            return op(eng, lambda e, o=out, x=a, y_=b, u=alu: e.tensor_tensor(out=o, in0=x, in1=y_, op=u), reads=reads, writes=writes)

        def ts(eng, out, a, s1, s2, op0, op1=None, reads=(), writes=()):
            def f(e, o=out, x=a, p=s1, q=s2, u=op0, v=op1):
                if v is None:
                    return e.tensor_scalar(out=o, in0=x, scalar1=p, scalar2=None, op0=u)
                return e.tensor_scalar(out=o, in0=x, scalar1=p, scalar2=q, op0=u, op1=v)
            return op(eng, f, reads=reads, writes=writes)

        def stt(eng, out, a, s, b, op0, op1, reads=(), writes=()):
            return op(eng, lambda e, o=out, x=a, p=s, y_=b, u=op0, v=op1: e.scalar_tensor_tensor(out=o, in0=x, scalar=p, in1=y_, op0=u, op1=v), reads=reads, writes=writes)

        dma(nw[:], normw[:, :], writes=["nw"])
        dma(fw[:], fnw.partition_broadcast(128), writes=["fw"])
        op("dve", lambda e: e.memset(onesb[:], 1.0), writes=["onesb"])
        x32f = x32[:].rearrange("p a b -> p (a b)")
        n = 0
        sgf = sg[:].rearrange("p a b -> p (a b)")
        for (src, dst, ncol, fold) in ((wg, Wg, 2048, True), (wba, Wba, D, False), (wbd, Wbd, D, False), (wo, Wo, D, False)):
            for kc in range(8):
                for c in range(0, ncol, 2048):
                    w_ = min(2048, ncol - c)
                    sl_ = n % 6
                    if sl_ < 2:
                        stg_, hb = x32f, sl_ * 2048
                        sk = [("x", 4 * sl_ + i) for i in range(4)]
                    else:
                        stg_, hb = sgf, (sl_ - 2) * 2048
                        sk = [("sg", 4 * (sl_ - 2) + i) for i in range(4)]
                    dma(stg_[:, hb:hb + w_], src[kc * 128:(kc + 1) * 128, c:c + w_], writes=sk, q=("sp", "pool", "act")[n % 3])
                    if n % 2 == 0:
                        if fold:
                            ts("dve", dst[:, kc, c:c + w_], stg_[:, hb:hb + w_], nw[:, kc:kc + 1], None, ALU.mult, reads=sk + ["nw"], writes=[id(dst)])
                        else:
                            op("dve", lambda e, d_=dst, kc=kc, c=c, w_=w_, hb=hb, stg_=stg_: e.tensor_copy(out=d_[:, kc, c:c + w_], in_=stg_[:, hb:hb + w_]), reads=sk, writes=[id(dst)])
                    else:
                        sc_ = nw[:, kc:kc + 1] if fold else 1.0
                        op("act", lambda e, d_=dst, kc=kc, c=c, w_=w_, hb=hb, sc_=sc_, stg_=stg_: e.activation(out=d_[:, kc, c:c + w_], in_=stg_[:, hb:hb + w_], func=AF.Copy, scale=sc_), reads=sk + ["nw"], writes=[id(dst)])
                    n += 1
        outs = []
        pidc = {}

        def rq(e):
            if 'r' not in pidc:
                pidc['r'] = e.partition_id() % 4
            return pidc['r']

        ccw = [(P1.streams[n_].sem, P1.streams[n_].ninc) for n_ in ("ccA", "ccD")]
        op("sp", lambda e: e.dma_start(out=ownA[:, :], in_=coutA[bass.ts(rq(e), 1024), :]), writes=["ownA"], stream="g:0", ext=ccw)
        op("sp", lambda e: e.dma_start(out=ownD[:, :], in_=coutD[bass.ts(rq(e), 1024), :]), writes=["ownD"], stream="g:1", ext=ccw)
        for s in range(NSB2):
            c0 = s * 512
            for kc in range(8):
                dma(x32[:, kc, :], xT2[kc * 128:(kc + 1) * 128, c0:c0 + 512], writes=[("x", kc)], q=("sp" if kc % 2 == 0 else "pool"))
                dma(oA[:, kc, :].bitcast(F32), ownA[kc * 128:(kc + 1) * 128, s * 256:(s + 1) * 256], reads=["ownA"], writes=[("oA", kc)])
                dma(oD[:, kc, :].bitcast(F32), ownD[kc * 128:(kc + 1) * 128, s * 256:(s + 1) * 256], reads=["ownD"], writes=[("oD", kc)], q="pool")
            for kc in range(8):
                act(sq[:, kc % 2, :], x32[:, kc, :], AF.Square, reads=[("x", kc)], writes=[("sq", kc % 2)])
                mm(B[0][:, :], onesb[:], sq[:, kc % 2, :], start=(kc == 0), stop=(kc == 7), reads=["onesb", ("sq", kc % 2)], writes=[("B", 0)])
            ts("dve", rstd[:], B[0][:, :], 1.0 / D, EPS, ALU.mult, ALU.add, reads=[("B", 0)], writes=["rstd"])
            act(rstd[:], rstd[:], AF.Ln, reads=["rstd"], writes=["rstd"]); act(rstd[:], rstd[:], AF.Exp, scale=-0.5, reads=["rstd"], writes=["rstd"])
            for kc in range(8):
                tt("dve" if kc % 2 == 0 else "pool", hnT[:, kc, :], x32[:, kc, :], rstd[:], ALU.mult, reads=[("x", kc), "rstd"], writes=["hnT"])
            for c in range(16):
                b = c % 2
                for kc in range(8):
                    mm(B[b][:, :], Wg[:, kc, c * 128:(c + 1) * 128], hnT[:, kc, :], start=(kc == 0), stop=(kc == 7), reads=[id(Wg), "hnT"], writes=[("B", b)])
                act(sg[:, c, :], B[b][:, :], AF.Sigmoid, reads=[("B", b)], writes=[("sg", c)])
            for dc in range(8):
                ba_, bd_ = (2, 3) if dc % 2 == 0 else (4, 5)
                for kc in range(8):
                    mm(B[ba_][:, :], Wba[:, kc, dc * 128:(dc + 1) * 128], oA[:, kc, :], start=(kc == 0), stop=(kc == 7), reads=[id(Wba), ("oA", kc)], writes=[("B", ba_)])
                for kc in range(8):
                    mm(B[bd_][:, :], Wbd[:, kc, dc * 128:(dc + 1) * 128], oD[:, kc, :], start=(kc == 0), stop=(kc == 7), reads=[id(Wbd), ("oD", kc)], writes=[("B", bd_)])
                tt("dve", tA[:], B[ba_][:, :], sg[:, dc, :], ALU.mult, reads=[("B", ba_), ("sg", dc)], writes=["tA"])
                tt("dve", junk[:, 0:512], B[bd_][:, :], sg[:, 8 + dc, :], ALU.mult, reads=[("B", bd_), ("sg", 8 + dc)], writes=["junk5"])
                tt("dve", mT[:, dc, :], junk[:, 0:512], tA[:], ALU.add, reads=["junk5", "tA"], writes=["mT"])
            for j in range(4):
                r0 = c0 + j * 128
                dma(xt[:], xtok[r0:r0 + 128, :], writes=["xt"])
                for hc in range(2):
                    bo_ = (j % 2) * 2 + hc
                    for dc in range(8):
                        mm(B[bo_][:, :], mT[:, dc, j * 128:(j + 1) * 128], Wo[:, dc, hc * 512:(hc + 1) * 512], start=(dc == 0), stop=(dc == 7), reads=["mT", id(Wo)], writes=[("B", bo_)])
                    tt("dve", h2[:, hc * 512:(hc + 1) * 512], B[bo_][:, :], xt[:, hc * 512:(hc + 1) * 512], ALU.add, reads=[("B", bo_), "xt"], writes=["h2"])
                op("pool", lambda e: e.memset(ss[:, 0:1], 0.0), writes=["ss0"])
                act(junk[:], h2[:], AF.Square, accum=ss[:, 0:1], reads=["h2"], writes=["junk5", "ss0"])
                ts("dve", ss[:, 0:1], ss[:, 0:1], 1.0 / D, EPS, ALU.mult, ALU.add, reads=["ss0"], writes=["ss0"])
                act(ss[:, 0:1], ss[:, 0:1], AF.Ln, reads=["ss0"], writes=["ss0"]); act(ss[:, 0:1], ss[:, 0:1], AF.Exp, scale=-0.5, reads=["ss0"], writes=["ss0"])
                stt("dve", yo[:], h2[:], ss[:, 0:1], fw[:], ALU.mult, ALU.mult, reads=["h2", "ss0", "fw"], writes=["yo"])
                outs.append(dma(y[r0:r0 + 128, :], yo[:], reads=["yo"]))
        P.final_wait("sp", outs)
        P.finalize(st, semstack)
    semstack.close()
    return nc


def _consts():
    c = np.zeros((128, 1024), np.float32)
    j = np.arange(128)[:, None]
    i = np.arange(128)[None, :]
    same = (j // 64) == (i // 64)
    c[:, 0:128] = np.eye(128, dtype=np.float32)
    c[:, 128:256] = ((i >= j) & same)
    c[:, 256:384] = ((i > j) & same)
    c[:, 384:512] = (i >= j)
    c[:, 512:1024] = (np.arange(512) % 64 != 0)[None, :]
    return c


def kernel(x, meta_tokens, norm_w, w_in, lambda_q1, lambda_k1, lambda_q2, lambda_k2,
           attn_norm_w, conv_w, a_log, dt_bias, dn_norm_w, w_branch_attn, w_branch_delta,
           w_out, final_norm_w):
    f = lambda a: np.ascontiguousarray(np.asarray(a, dtype=np.float32))
    x = f(x)
    Bn, S, _ = x.shape
    NSB = S // 512
    w = f(w_in)[0]
    offs = np.cumsum([0, 1024, 1024, 1024, 1024, 1024, 1024, 1024, 1024, 8, 8, 1024, 1024])
    seg = lambda i: w[:, offs[i]:offs[i + 1]]
    aq, ak, av, az, dq_, dk_, dv_, dz, db, da, ga, gd = [seg(i) for i in range(12)]
    cwf = f(conv_w)[0]
    nwl = f(norm_w)[0].reshape(8, 128).T.copy()
    metaT = np.zeros((D, 128), np.float32)
    metaT[:, 112:] = f(meta_tokens).T
    lamv = np.concatenate([f(lambda_q1)[0], f(lambda_k1)[0], f(lambda_q2)[0], f(lambda_k2)[0]])[None, :].copy()
    consts = _consts()
    xTs = [np.ascontiguousarray(x[b].T) for b in range(Bn)]
    maps1 = []
    for core in range(8):
        b, hp = core // 4, core % 4
        hs = [2 * hp, 2 * hp + 1]
        col = lambda m, h: m[:, h * 128:(h + 1) * 128]
        wfm = np.concatenate([col(m, h) for m in (aq, ak, az, dq_, dk_, dv_, dz) for h in hs], axis=1)
        wbgl = np.stack([db[:, hs[0]], db[:, hs[1]], da[:, hs[0]], da[:, hs[1]]], axis=1)
        wavl = np.concatenate([col(av, h) for h in hs], axis=1)
        cwl = np.zeros((128, 24), np.float32)
        for j in range(3):
            for hh in range(2):
                ch0 = j * 1024 + hs[hh] * 128
                cwl[:, (2 * j + hh) * 4:(2 * j + hh) * 4 + 4] = cwf[:, ch0:ch0 + 128].T
        maps1.append(dict(
            xT=xTs[b], metaT=metaT, wfm=np.ascontiguousarray(wfm), wbg=np.ascontiguousarray(wbgl),
            wav=np.ascontiguousarray(wavl), normw=nwl, convw=cwl,
            alog=f(a_log)[0][hs][None, :].copy(), dtb=f(dt_bias)[0][hs][None, :].copy(), lamv=lamv,
            anw=f(attn_norm_w)[0][:, None].copy(), dnw=f(dn_norm_w)[0][:, None].copy(), consts=consts))
    NQ = S // 4
    wgl = np.ascontiguousarray(np.concatenate([ga, gd], axis=1))
    for core in range(8):
        b, r = core // 4, core % 4
        sl = slice(r * NQ, (r + 1) * NQ)
        maps1[core].update(dict(
            xT2=np.ascontiguousarray(xTs[b][:, sl]), xtok=np.ascontiguousarray(x[b, sl]),
            wg=wgl, wba=f(w_branch_attn)[0], wbd=f(w_branch_delta)[0], wo=f(w_out)[0],
            fnw=f(final_norm_w)[None, :].copy()))
    nc = build_fused(NSB)
    res = run_bass_kernel_spmd(nc, maps1, core_ids=list(range(8))).results
    out = np.zeros((Bn, S, D), np.float32)
    for core in range(8):
        b, r = core // 4, core % 4
        out[b, r * NQ:(r + 1) * NQ] = np.asarray(res[core]["y"])
    return out
```

```python
import math
import numpy as np
import ml_dtypes
from contextlib import ExitStack
import concourse.bass as bass
import concourse.mybir as mybir
from concourse.bass_utils import run_bass_kernel_spmd

F32 = mybir.dt.float32
BF16 = mybir.dt.bfloat16
AF = mybir.ActivationFunctionType
ALU = mybir.AluOpType
AX = mybir.AxisListType
NPBF = ml_dtypes.bfloat16

D = 1024
NMETA = 16
EPS = 1e-6
LAM_INIT = 0.8 - 0.6 * math.exp(-0.3 * 0)
ENGS = ("pe", "act", "dve", "pool", "sp")
SAME_ENGINE_SYNC = True


class Stream:
    def __init__(self, name, step):
        self.name, self.step, self.sem, self.nops, self.ninc = name, step, None, 0, 0


class Op:
    __slots__ = ("issuer", "stream", "emit", "waits", "inc", "idx", "value", "clock", "ext")

    def __init__(self, issuer, stream, emit):
        self.issuer, self.stream, self.emit = issuer, stream, emit
        self.waits, self.inc, self.idx, self.value, self.clock = [], False, 0, 0, None
        self.ext = ()


class Prog:
    def __init__(self, nc, tag="s"):
        self.nc = nc
        self.tag = tag
        self.streams = {e: Stream(e, 1) for e in ENGS}
        self.ops = {e: [] for e in ENGS}
        self.known = {e: {} for e in ENGS}
        self.last_writer, self.readers, self.all_ops = {}, {}, []

    def op(self, issuer, emit, reads=(), writes=(), stream=None, after=(), ext=()):
        if stream is None:
            st = self.streams[issuer]
        else:
            st = self.streams.setdefault(stream, Stream(stream, 1 if stream.startswith("cc") else 16))
        o = Op(issuer, st, emit)
        o.ext = tuple(ext)
        if st.step == 1 and st.name.startswith("cc"):
            o.inc = True
        st.nops += 1
        o.idx = st.nops
        deps = list(after)
        for r in reads:
            w = self.last_writer.get(r)
            if w is not None:
                deps.append(w)
        for w_ in writes:
            w = self.last_writer.get(w_)
            if w is not None:
                deps.append(w)
            deps.extend(self.readers.get(w_, ()))
        known = self.known[issuer]
        best = {}
        for d in deps:
            sname = d.stream.name
            if sname == issuer and (issuer == "pe" or not SAME_ENGINE_SYNC):
                continue
            if known.get(sname, 0) >= d.idx:
                continue
            b = best.get(sname)
            if b is None or d.idx > b.idx:
                best[sname] = d
        for d in best.values():
            o.waits.append(d)
            d.inc = True
            for k, v in d.clock.items():
                if known.get(k, 0) < v:
                    known[k] = v
        clock = dict(known)
        clock[st.name] = max(clock.get(st.name, 0), o.idx)
        o.clock = clock
        for r in reads:
            self.readers.setdefault(r, []).append(o)
        for w_ in writes:
            self.last_writer[w_] = o
            self.readers[w_] = []
        self.ops[issuer].append(o)
        self.all_ops.append(o)
        return o

    def final_wait(self, issuer, ops):
        o = Op(issuer, self.streams[issuer], lambda e: None)
        for d in ops:
            o.waits.append(d)
            d.inc = True
        o.clock = {}
        self.ops[issuer].append(o)

    def barrier(self):
        last = {}
        for o in self.all_ops:
            if not o.stream.name.startswith("cc"):
                last[o.stream.name] = o
        for e in ENGS:
            o = Op(e, self.streams[e], lambda eng: None)
            for d in last.values():
                o.waits.append(d)
                d.inc = True
            o.clock = {}
            self.ops[e].append(o)

    def finalize(self, stack, semstack=None):
        nc = self.nc
        semstack = semstack or stack
        for o in self.all_ops:
            if o.inc:
                s = o.stream
                s.ninc += 1
                o.value = s.ninc * s.step
        for s in self.streams.values():
            if s.ninc > 0:
                s.sem = semstack.enter_context(nc.semaphore(self.tag + "_" + s.name.replace(":", "_")))
        block = stack.enter_context(nc.Block())
        prog = self

        def replay(eng, key):
            for o in prog.ops[key]:
                for d in o.waits:
                    eng.wait_ge(d.stream.sem, d.value)
                for (sem_, val_) in o.ext:
                    eng.wait_ge(sem_, val_)
                ins = o.emit(eng)
                if o.inc:
                    if o.stream.step == 1 and o.stream.name.startswith("cc"):
                        ins.then_inc(o.stream.sem)
                    else:
                        ins.then_inc(o.stream.sem, o.stream.step)

        block.tensor(lambda e: replay(e, "pe"))
        block.scalar(lambda e: replay(e, "act"))
        block.vector(lambda e: replay(e, "dve"))
        block.gpsimd(lambda e: replay(e, "pool"))
        block.sync(lambda e: replay(e, "sp"))


def build_fused(NSB):
    NT = NSB * 512
    NKT = 1 + NSB * 4
    NSB2 = NSB // 4
    NT2 = NSB2 * 512
    nc = bass.Bass("TRN2", target_bir_lowering=False)
    dt_in = lambda n, s, d=F32: nc.dram_tensor(n, s, d, kind="ExternalInput").ap()
    xT2 = dt_in("xT2", [D, NT2])
    xtok = dt_in("xtok", [NT2, D])
    wg = dt_in("wg", [D, 2048])
    wba = dt_in("wba", [D, D])
    wbd = dt_in("wbd", [D, D])
    wo = dt_in("wo", [D, D])
    fnw = dt_in("fnw", [1, D])
    y = nc.dram_tensor("y", [NT2, D], F32, kind="ExternalOutput").ap()
    cinA_t = [nc.dram_tensor("cinA%d" % c, [256, NT2 // 2], F32) for c in range(4)]
    cinD_t = [nc.dram_tensor("cinD%d" % c, [256, NT2 // 2], F32) for c in range(4)]
    coutA_t = nc.dram_tensor("coutA", [4 * 1024, NT2 // 2], F32)
    coutD_t = nc.dram_tensor("coutD", [4 * 1024, NT2 // 2], F32)
    cinA = [t.ap() for t in cinA_t]
    cinD = [t.ap() for t in cinD_t]
    coutA, coutD = coutA_t.ap(), coutD_t.ap()
    GROUPS = [[0, 1, 2, 3], [4, 5, 6, 7]]
    ownA = nc.dram_tensor("ownA", [1024, NT2 // 2], F32).ap()
    ownD = nc.dram_tensor("ownD", [1024, NT2 // 2], F32).ap()
    semstack = ExitStack()
    xT = dt_in("xT", [D, NT])
    metaT = dt_in("metaT", [D, 128])
    wfm = dt_in("wfm", [D, 1792])
    wbg = dt_in("wbg", [D, 4])
    wav = dt_in("wav", [D, 256])
    normw = dt_in("normw", [128, 8])
    convw = dt_in("convw", [128, 24])
    alog = dt_in("alog", [1, 2])
    dtb = dt_in("dtb", [1, 2])
    lamv = dt_in("lamv", [1, 256])
    anw = dt_in("anw", [128, 1])
    dnw = dt_in("dnw", [128, 1])
    consts = dt_in("consts", [128, 1024])

    with ExitStack() as st:
        sb_ = lambda n, s, d=F32: st.enter_context(nc.sbuf_tensor(n, s, d))
        W_fm = sb_("W_fm", [128, 8, 2304], BF16)
        W_av = sb_("W_av", [128, 8, 256], BF16)
        KT = sb_("KT", [128, 2, NMETA + NT], BF16)
        V1 = sb_("V1", [128, 2, NKT, 129], BF16)
        x32 = sb_("x32", [128, 8, 512])
        sq = sb_("sq", [128, 2, 512], BF16)
        rstd = sb_("rstd", [128, 512])
        hnT = sb_("hnT", [128, 8, 512], BF16)
        QT = sb_("QT", [128, 2, 512], BF16)
        saz = sb_("saz", [128, 2, 512], BF16)
        sdz2 = sb_("sdz", [128, 2, 512], BF16)
        pre = sb_("pre", [128, 3, 515])
        cv = sb_("cv", [128, 3, 512])
        hist = sb_("hist", [128, 6, 3])
        beta_b = sb_("beta_b", [128, 512])
        g_b = sb_("g_b", [128, 512])
        gc_b = sb_("gc_b", [128, 512])
        egc2 = sb_("egc_b", [128, 2, 512])
        ekd_b = sb_("ekd_b", [128, 512])
        tmpf = sb_("tmpf", [128, 512])
        qnT = sb_("qnT", [128, 512], BF16)
        knT = sb_("knT", [128, 512], BF16)
        qdT2 = sb_("qdT", [128, 2, 512], BF16)
        kbgT = sb_("kbgT", [128, 512], BF16)
        kdT = sb_("kdT", [128, 512], BF16)
        vbT = sb_("vbT", [128, 512], BF16)
        PT = sb_("PT", [128, 2, 2, 512], BF16)
        ozA_s = sb_("ozA_s", [128, 2, 512], BF16)
        ozD2 = sb_("ozD_s", [128, 2, 512], BF16)
        cst = sb_("cst", [128, 1024])
        identb = sb_("identb", [128, 128], BF16)
        onesb = sb_("onesb", [128, 128], BF16)
        causb = sb_("causb", [128, 128], BF16)
        nw = sb_("nw", [128, 8])
        cw = sb_("cw", [128, 24])
        wbg32 = sb_("wbg32", [128, 8, 4])
        sm = sb_("sm", [128, 32])
        lamt = sb_("lamt", [128, 256])
        lamp = sb_("lamp", [128, 128])
        S32 = sb_("S32", [128, 2, 128])
        Sb = sb_("Sb", [128, 2, 128], BF16)
        junk = sb_("junk", [128, 128])
        junk2 = sb_("junk2", [128, 128])
        MA = sb_("MA", [128, 4, 256], BF16)
        kdm = sb_("kdm", [128, 4, 128], BF16)
        um = sb_("um", [128, 4, 128])
        wTm = sb_("wTm", [128, 4, 128], BF16)
        fz = sb_("fz", [128, 2])
        gcol = sb_("gcol", [128, 4])
        zer = sb_("zer", [128, 128])
        vnew = sb_("vnew", [128, 128], BF16)
        ot = sb_("ot", [128, 128])
        o1 = sb_("o1", [128, 4, 128])
        oo = sb_("oo", [128, 4, 128])
        junke = sb_("junke", [128, 4, 128])
        onbe = sb_("onbe", [128, 4, 128], BF16)
        sse = sb_("sse", [128, 16])
        onb = sb_("onb", [128, 128], BF16)
        ss = sb_("ss", [128, 8])
        SA = st.enter_context(nc.psum_tensor("SA", [128, 2, 512], F32))
        SB = st.enter_context(nc.psum_tensor("SB", [128, 2, 512], F32))
        ACC = [st.enter_context(nc.psum_tensor("ACC%d" % i, [128, 512], F32)) for i in range(3)]
        B = [SA[:, 0, :], SA[:, 1, :], SB[:, 0, :], ACC[0][:], ACC[1][:], ACC[2][:], SB[:, 1, :]]
        SBANK = [(SA, (0, 1)), (SB, (2, 6))]
        pTr = st.enter_context(nc.psum_tensor("pTr", [128, 1024], BF16))

        identf = cst[:, 0:128]
        inclT = cst[:, 128:256]
        strictT = cst[:, 256:384]
        resetm = cst[:, 512:1024]
        P = Prog(nc, "p1")
        sink = [None]

        def op(issuer, emit, reads=(), writes=(), stream=None, after=()):
            if sink[0] is not None:
                sink[0].append((issuer, emit, tuple(reads), tuple(writes)))
                return None
            return P.op(issuer, emit, reads=reads, writes=writes, stream=stream, after=after)

        def emit_list(lst, extra_reads=()):
            for it_ in lst:
                if it_[0] == "dma":
                    _, q_, o_, i_, r_, w_ = it_
                    dma(o_, i_, reads=tuple(r_) + tuple(extra_reads), writes=w_, q=q_)
                else:
                    (i_, e_, r_, w_) = it_
                    P.op(i_, e_, reads=tuple(r_) + tuple(extra_reads), writes=w_)

        TX = "x32tok"

        def fence():
            op("pool", lambda e: e.memset(fz[:, 0:1], 0.0), writes=[TX])

        def interleave(a_, b_):
            out_, ia, ib = [], 0, 0
            while ia < len(a_) or ib < len(b_):
                if ib >= len(b_) or (ia < len(a_) and ia * len(b_) <= ib * len(a_)):
                    out_.append(a_[ia]); ia += 1
                else:
                    out_.append(b_[ib]); ib += 1
            return out_
        dq = [0]

        dlast = {}

        def dma(out, in_, reads=(), writes=(), q=None):
            q = q or "sp"
            if sink[0] is not None:
                sink[0].append(("dma", q, out, in_, tuple(reads), tuple(writes)))
                return None
            dq[0] += 1
            name = "%s:%d" % (q, dq[0] % 8)
            prev = dlast.get(name)
            o_ = op(q, lambda e, o=out, i=in_: e.dma_start(out=o, in_=i), reads=reads, writes=writes, stream=name, after=([prev] if prev is not None else ()))
            dlast[name] = o_
            return o_

        def mm(out, lhsT, rhs, start=True, stop=True, reads=(), writes=()):
            return op("pe", lambda e, o=out, l=lhsT, r=rhs, s0=start, s1=stop: e.matmul(o, lhsT=l, rhs=r, start=s0, stop=s1, skip_group_check=True), reads=reads, writes=writes)

        def tr(out, in_, reads=(), writes=()):
            return op("pe", lambda e, o=out, i=in_: e.transpose(o, i, identb[:]), reads=list(reads) + ["identb"], writes=writes)

        def act(out, in_, func, scale=1.0, bias=None, accum=None, reads=(), writes=()):
            def f(e, o=out, i=in_, fn=func, s=scale, b=bias, a=accum):
                kw = {}
                if b is not None:
                    kw["bias"] = b
                if a is not None:
                    kw["accum_out"] = a
                return e.activation(out=o, in_=i, func=fn, scale=s, **kw)
            return op("act", f, reads=reads, writes=writes)

        def ve(eng, fn, reads=(), writes=()):
            return op(eng, fn, reads=reads, writes=writes)

        def tt(eng, out, a, b, alu, reads=(), writes=()):
            return op(eng, lambda e, o=out, x=a, y=b, u=alu: e.tensor_tensor(out=o, in0=x, in1=y, op=u), reads=reads, writes=writes)

        def ts(eng, out, a, s1, s2, op0, op1=None, reads=(), writes=()):
            def f(e, o=out, x=a, p=s1, q=s2, u=op0, v=op1):
                if v is None:
                    return e.tensor_scalar(out=o, in0=x, scalar1=p, scalar2=None, op0=u)
                return e.tensor_scalar(out=o, in0=x, scalar1=p, scalar2=q, op0=u, op1=v)
            return op(eng, f, reads=reads, writes=writes)

        def stt(eng, out, a, s, b, op0, op1, reads=(), writes=()):
            return op(eng, lambda e, o=out, x=a, p=s, y=b, u=op0, v=op1: e.scalar_tensor_tensor(out=o, in0=x, scalar=p, in1=y, op0=u, op1=v), reads=reads, writes=writes)

        dma(cst[:], consts[:, :], writes=["cst"])
        dma(nw[:], normw[:, :], writes=["nw"])
        dma(cw[:], convw[:, :], writes=["cw"])
        dma(wbg32[:], wbg.rearrange("(k p) c -> p k c", p=128), writes=["wbg32"])
        dma(sm[:, 4:6], alog.partition_broadcast(128), writes=["sm_a"])
        dma(sm[:, 6:8], dtb.partition_broadcast(128), writes=["sm_d"])
        dma(sm[:, 8:9], anw[:, :], writes=["sm_anw"])
        dma(sm[:, 9:10], dnw[:, :], writes=["sm_dnw"])
        dma(lamt[:], lamv.partition_broadcast(128), writes=["lamt"])
        ve("dve", lambda e: e.tensor_copy(out=identb[:], in_=identf), reads=["cst"], writes=["identb"])
        ve("dve", lambda e: e.tensor_copy(out=causb[:], in_=cst[:, 384:512]), reads=["cst"], writes=["causb"])
        ve("dve", lambda e: e.memset(onesb[:], 1.0), writes=["onesb"])
        ve("pool", lambda e: e.memset(V1[:, :, :, 128:129], 1.0), writes=["V1"])
        ve("pool", lambda e: e.memset(hist[:], 0.0), writes=["hist"])
        ve("pool", lambda e: e.memset(S32[:], 0.0), writes=["S32_0", "S32_1"])
        ve("pool", lambda e: e.memset(Sb[:], 0.0), writes=["Sb_0", "Sb_1"])
        ve("pool", lambda e: e.memset(zer[:], 0.0), writes=["zer"])
        ve("pool", lambda e: e.memset(vnew[:], 0.0), writes=["vnew"])
        ve("pool", lambda e: e.memset(ot[:], 0.0), writes=["ot"])
        tt("dve", lamp[:, 0:64], lamt[:, 0:64], lamt[:, 64:128], ALU.mult, reads=["lamt"], writes=["lamp"])
        tt("dve", lamp[:, 64:128], lamt[:, 128:192], lamt[:, 192:256], ALU.mult, reads=["lamt"], writes=["lamp"])
        ve("dve", lambda e: e.reduce_sum(out=sm[:, 2:3], in_=lamp[:, 0:64], axis=AX.X), reads=["lamp"], writes=["sm_s"])
        ve("dve", lambda e: e.reduce_sum(out=sm[:, 3:4], in_=lamp[:, 64:128], axis=AX.X), reads=["lamp"], writes=["sm_s"])
        act(sm[:, 2:4], sm[:, 2:4], AF.Exp, reads=["sm_s"], writes=["sm_s"])
        tt("dve", sm[:, 0:1], sm[:, 2:3], sm[:, 3:4], ALU.subtract, reads=["sm_s"], writes=["sm_lam"])
        ts("dve", sm[:, 1:2], sm[:, 0:1], -1.0, -LAM_INIT, ALU.mult, ALU.add, reads=["sm_lam"], writes=["sm_lam"])
        ts("dve", sm[:, 8:9], sm[:, 8:9], 1.0 - LAM_INIT, None, ALU.mult, reads=["sm_anw"], writes=["sm_anw"])
        act(sm[:, 4:6], sm[:, 4:6], AF.Exp, reads=["sm_a"], writes=["sm_a"])
        ts("dve", sm[:, 4:6], sm[:, 4:6], -1.0, None, ALU.mult, reads=["sm_a"], writes=["sm_a"])
        x32f = x32[:].rearrange("p a b -> p (a b)")
        for kc in range(8):
            hb = (kc % 2) * 2048
            sk = [("x", 4 * (kc % 2) + i) for i in range(4)]
            dma(x32f[:, hb:hb + 1792], wfm[kc * 128:(kc + 1) * 128, :], writes=sk)
            dma(x32f[:, hb + 1792:hb + 2048], wav[kc * 128:(kc + 1) * 128, :], reads=sk, writes=[("wavd", kc)], q="pool")
            ts("dve", W_fm[:, kc, 0:1792], x32f[:, hb:hb + 1792], nw[:, kc:kc + 1], None, ALU.mult, reads=sk + ["nw"], writes=["W_fm"])
            act(W_av[:, kc, :], x32f[:, hb + 1792:hb + 2048], AF.Copy, scale=nw[:, kc:kc + 1], reads=sk + ["nw", ("wavd", kc)], writes=["W_av"])
            for j in range(4):
                ts("dve", W_fm[:, kc, 1792 + j * 128:1920 + j * 128], wbg32[:, kc, j:j + 1].to_broadcast([128, 128]), nw[:, kc:kc + 1], None, ALU.mult, reads=["wbg32", "nw"], writes=["W_fm"])
        bank = [0]

        ipb = [(0, 1)]

        def nextbank():
            bank[0] ^= 1
            return ipb[0][bank[0]]

        def inproj_fm(c, T):
            b = nextbank()
            for kc in range(8):
                mm(B[b][:, 0:T], W_fm[:, kc, c * 128:(c + 1) * 128], hnT[:, kc, 0:T], start=(kc == 0), stop=(kc == 7), reads=["W_fm", "hnT"], writes=[("B", b)])
            return b

        def superblock(sbi, prevRC):
            meta = sbi < 0
            T = 128 if meta else 512
            src = metaT if meta else xT
            c0 = 0 if meta else sbi * 512
            sink[0] = []
            fence()
            for kc in range(8):
                dma(x32[:, kc, 0:T], src[kc * 128:(kc + 1) * 128, c0:c0 + T], reads=[TX], writes=[("x", kc)], q=("sp" if kc % 2 == 0 else "pool"))
            for kc in range(8):
                act(sq[:, kc % 2, 0:T], x32[:, kc, 0:T], AF.Square, reads=[("x", kc), TX], writes=[("sq", kc % 2)])
                mm(B[2][:, 0:T], onesb[:], sq[:, kc % 2, 0:T], start=(kc == 0), stop=(kc == 7), reads=["onesb", ("sq", kc % 2)], writes=[("B", 2)])
            ts("dve", rstd[:, 0:T], B[2][:, 0:T], 1.0 / D, EPS, ALU.mult, ALU.add, reads=[("B", 2)], writes=["rstd"])
            act(rstd[:, 0:T], rstd[:, 0:T], AF.Ln, reads=["rstd"], writes=["rstd"]); act(rstd[:, 0:T], rstd[:, 0:T], AF.Exp, scale=-0.5, reads=["rstd"], writes=["rstd"])
            for kc in range(8):
                tt("dve" if kc % 2 == 0 else "pool", hnT[:, kc, 0:T], x32[:, kc, 0:T], rstd[:, 0:T], ALU.mult, reads=[("x", kc), "rstd", TX], writes=["hnT"])
            for h in range(2):
                koff = 0 if meta else NMETA + sbi * 512
                if not meta:
                    b = inproj_fm(0 + h, T)
                    act(QT[:, h, 0:T], B[b][:, 0:T], AF.Copy, reads=[("B", b)], writes=[("QT", h)])
                b = inproj_fm(2 + h, T)
                if meta:
                    ve("dve", lambda e, b=b, h=h: e.tensor_copy(out=KT[:, h, 0:16], in_=B[b][:, 112:128]), reads=[("B", b)], writes=[("KT", h, -1)])
                else:
                    ve("dve", lambda e, b=b, h=h, k=koff: e.tensor_copy(out=KT[:, h, k:k + 512], in_=B[b][:, 0:512]), reads=[("B", b)], writes=[("KT", h, sbi)])
                if not meta:
                    b = inproj_fm(4 + h, T)
                    act(saz[:, h, 0:T], B[b][:, 0:T], AF.Silu, reads=[("B", b)], writes=[("saz", h)])
            ntile = 1 if meta else 4
            for j in range(ntile):
                b = nextbank()
                for kc in range(8):
                    lh = hnT[:, kc, 112:128] if meta else hnT[:, kc, j * 128:(j + 1) * 128]
                    np_ = 16 if meta else 128
                    mm(B[b][0:np_, 0:256], lh, W_av[:, kc, :], start=(kc == 0), stop=(kc == 7), reads=["hnT", "W_av"], writes=[("B", b)])
                kt = 0 if meta else 1 + sbi * 4 + j
                for h in range(2):
                    np_ = 16 if meta else 128
                    ve("dve", lambda e, b=b, h=h, kt=kt, n=np_: e.tensor_copy(out=V1[0:n, h, kt, 0:128], in_=B[b][0:n, h * 128:(h + 1) * 128]), reads=[("B", b)], writes=[("V1", h, kt)])
            AB = sink[0]
            sink[0] = None
            emit_list(interleave(AB, prevRC))
            if sbi > 0 and sbi % NSB2 == 0:
                collect(sbi // NSB2 - 1)
            EPI1 = []
            if not meta:
                for h in range(2):
                    nkt = 1 + 4 * sbi + 4

                    def att_qk(kt):
                        nk = 16 if kt == 0 else 128
                        ks = 0 if kt == 0 else NMETA + (kt - 1) * 128
                        a = kt - 1 - 4 * sbi
                        diag = a >= 0
                        qlo = a * 128 if diag else 0
                        buf = kt % 2
                        ksb = -1 if kt == 0 else (kt - 1) // 4
                        Sps, (k0, k1) = SBANK[buf]
                        for m in range(2):
                            mm(Sps[0:nk, m, qlo:512], KT[m * 64:(m + 1) * 64, h, ks:ks + nk], QT[m * 64:(m + 1) * 64, h, qlo:512], reads=[("KT", h, ksb), ("QT", h)], writes=[("B", (k0, k1)[m])])
                        act(PT[0:nk, buf, :, qlo:512], Sps[0:nk, :, qlo:512], AF.Exp, scale=0.125, reads=[("B", k0), ("B", k1)], writes=[("PT", buf, 0), ("PT", buf, 1)])
                        for m in range(2):
                            if diag:
                                tt("dve", PT[:, buf, m, qlo:qlo + 128], PT[:, buf, m, qlo:qlo + 128], causb[:], ALU.mult, reads=[("PT", buf, m), "causb"], writes=[("PT", buf, m)])

                    def att_pv(kt):
                        nk = 16 if kt == 0 else 128
                        a = kt - 1 - 4 * sbi
                        buf = kt % 2
                        for qi in range(max(a, 0), 4):
                            for m in range(2):
                                last = (kt == 4 * sbi + qi + 1)
                                r_ = qi * 2 + m
                                bk, off = 3 + r_ // 3, (r_ % 3) * 129
                                mm(B[bk][:, off:off + 129], PT[0:nk, buf, m, qi * 128:(qi + 1) * 128], V1[0:nk, h, kt, 0:129], start=(kt == 0 and r_ % 3 == 0), stop=last, reads=[("PT", buf, m), ("V1", h, kt)], writes=[("B", bk)])

                    for step in range(nkt + 1):
                        if step < nkt:
                            att_qk(step)
                        if step >= 1:
                            att_pv(step - 1)
                    qchains = []
                    for qi in range(4):
                        sink[0] = []
                        b0_, f0_ = 3 + (qi * 2) // 3, ((qi * 2) % 3) * 129
                        b1_, f1_ = 3 + (qi * 2 + 1) // 3, ((qi * 2 + 1) % 3) * 129
                        a0_, a1_ = B[b0_][:, f0_:f0_ + 129], B[b1_][:, f1_:f1_ + 129]
                        kk = ("B", b0_)
                        kk1 = ("B", b1_)
                        q4 = qi * 4
                        kq = lambda n, qi=qi: (n, qi)
                        o1q, ooq, jq, onq = o1[:, qi, :], oo[:, qi, :], junke[:, qi, :], onbe[:, qi, :]
                        rgn = slice(qi * 256, qi * 256 + 128)
                        kr = "pTr%d" % (2 * qi)
                        ve("dve", lambda e, a0_=a0_, q4=q4: e.reciprocal(out=sse[:, q4:q4 + 1], in_=a0_[:, 128:129]), reads=[kk], writes=[kq("s0")])
                        ve("dve", lambda e, a1_=a1_, q4=q4: e.reciprocal(out=sse[:, q4 + 1:q4 + 2], in_=a1_[:, 128:129]), reads=[kk1], writes=[kq("s1")])
                        tt("dve", sse[:, q4 + 2:q4 + 3], sse[:, q4 + 1:q4 + 2], sm[:, 1:2], ALU.mult, reads=[kq("s1"), "sm_lam"], writes=[kq("s2")])
                        ts("dve", o1q, a0_[:, 0:128], sse[:, q4:q4 + 1], None, ALU.mult, reads=[kk, kq("s0")], writes=[kq("o1")])
                        stt("dve", ooq, a1_[:, 0:128], sse[:, q4 + 2:q4 + 3], o1q, ALU.mult, ALU.add, reads=[kk1, kq("s2"), kq("o1")], writes=[kq("oo")])
                        ve("pool", lambda e, q4=q4: e.memset(sse[:, q4 + 3:q4 + 4], 0.0), writes=[kq("s3")])
                        act(jq, ooq, AF.Square, accum=sse[:, q4 + 3:q4 + 4], reads=[kq("oo")], writes=[kq("junke"), kq("s3")])
                        ts("dve", sse[:, q4 + 3:q4 + 4], sse[:, q4 + 3:q4 + 4], 1.0 / 128, EPS, ALU.mult, ALU.add, reads=[kq("s3")], writes=[kq("s3")])
                        act(sse[:, q4 + 3:q4 + 4], sse[:, q4 + 3:q4 + 4], AF.Ln, reads=[kq("s3")], writes=[kq("s3")])
                        act(sse[:, q4 + 3:q4 + 4], sse[:, q4 + 3:q4 + 4], AF.Exp, scale=-0.5, reads=[kq("s3")], writes=[kq("s3")])
                        ts("dve", onq, ooq, sse[:, q4 + 3:q4 + 4], None, ALU.mult, reads=[kq("oo"), kq("s3")], writes=[kq("onbe")])
                        tr(pTr[:, rgn], onq, reads=[kq("onbe")], writes=[kr, "pTrbank"])
                        stt("dve", ozA_s[:, h, qi * 128:(qi + 1) * 128], pTr[:, rgn], sm[:, 8:9], saz[:, h, qi * 128:(qi + 1) * 128], ALU.mult, ALU.mult, reads=["pTrbank", kr, "sm_anw", ("saz", h)], writes=[("ozA_s", h, qi)])
                        qchains.append(sink[0])
                    epl = []
                    for l_ in qchains:
                        epl = interleave(epl, l_)
                    sink[0] = [] if h == 1 else None
                    if h == 0:
                        emit_list(epl)
                    else:
                        sink[0] = list(epl)
                    dma(cinA[sbi // NSB2][h * 128:(h + 1) * 128, (sbi % NSB2) * 256:(sbi % NSB2) * 256 + 256], ozA_s[:, h, :].bitcast(F32), reads=[("ozA_s", h, q_) for q_ in range(4)], writes=[("cinA", h, sbi)])
                    if h == 1:
                        EPI1 = sink[0]
                        sink[0] = None
            Wl, PRl, RCl = [], [], []
            for h in range(2):
                sdz, egc_b, qdT, ozD_s = sdz2[:, h, :], egc2[:, h, :], qdT2[:, h, :], ozD2[:, h, :]
                kE, kQ, kZ, kO = ("egc_b", h), ("qdT", h), ("sdz", h), ("ozD_s", h)
                sink[0] = []
                if h == 1:
                    ipb[0] = (3, 4)
                for j in range(3):
                    b = inproj_fm(6 + 2 * j + h, T)
                    ve("pool", lambda e, j=j, h=h: e.tensor_copy(out=pre[:, j, 0:3], in_=hist[:, h * 3 + j, :]), reads=["hist"], writes=[("pre", j)])
                    act(pre[:, j, 3:3 + T], B[b][:, 0:T], AF.Copy, reads=[("B", b)], writes=[("pre", j)])
                if not meta:
                    b = inproj_fm(12 + h, T)
                    act(sdz[:, 0:T], B[b][:, 0:T], AF.Silu, reads=[("B", b)], writes=[kZ])
                b = inproj_fm(16 + h, T)
                act(g_b[:, 0:T], B[b][:, 0:T], AF.Exp, bias=sm[:, 6 + h:7 + h], reads=[("B", b), "sm_d"], writes=["g_b"])
                act(g_b[:, 0:T], g_b[:, 0:T], AF.Ln, bias=1.0, reads=["g_b"], writes=["g_b"])
                ipb[0] = (0, 1)
                wsplit = len(sink[0])
                b = inproj_fm(14 + h, T)
                act(beta_b[:, 0:T], B[b][:, 0:T], AF.Sigmoid, reads=[("B", b)], writes=["beta_b"])
                for j in range(3):
                    w = (2 * j + h) * 4
                    ts("dve", cv[:, j, 0:T], pre[:, j, 0:T], cw[:, w:w + 1], None, ALU.mult, reads=[("pre", j), "cw"], writes=[("cv", j)])
                    for tap in range(1, 4):
                        stt("dve", cv[:, j, 0:T], pre[:, j, tap:tap + T], cw[:, w + tap:w + tap + 1], cv[:, j, 0:T], ALU.mult, ALU.add, reads=[("pre", j), "cw", ("cv", j)], writes=[("cv", j)])
                    ve("pool", lambda e, j=j, h=h, T=T: e.tensor_copy(out=hist[:, h * 3 + j, :], in_=pre[:, j, T:T + 3]), reads=[("pre", j)], writes=["hist"])
                ts("dve", g_b[:, 0:T], g_b[:, 0:T], sm[:, 4 + h:5 + h], None, ALU.mult, reads=["g_b", "sm_a"], writes=["g_b"])
                ve("dve", lambda e, T=T: e.tensor_tensor_scan(out=gc_b[:, 0:T], data0=resetm[:, 0:T], data1=g_b[:, 0:T], initial=0.0, op0=ALU.mult, op1=ALU.add), reads=["cst", "g_b"], writes=["gc_b"])
                for j in range(3):
                    act(cv[:, j, 0:T], cv[:, j, 0:T], AF.Silu, reads=[("cv", j)], writes=[("cv", j)])
                act(egc_b[:, 0:T], gc_b[:, 0:T], AF.Exp, reads=["gc_b"], writes=[kE])
                scr = (tmpf, ekd_b)
                bnk = (2, 6)
                for j in range(2):
                    act(sq[:, j, 0:T], cv[:, j, 0:T], AF.Square, reads=[("cv", j)], writes=[("sq", j)])
                for j in range(2):
                    mm(B[bnk[j]][:, 0:T], onesb[:], sq[:, j, 0:T], reads=["onesb", ("sq", j)], writes=[("B", bnk[j])])
                for j in range(2):
                    act(scr[j][:, 0:T], B[bnk[j]][:, 0:T], AF.Ln, bias=1e-6, reads=[("B", bnk[j])], writes=[("scr", j)])
                for j in range(2):
                    act(scr[j][:, 0:T], scr[j][:, 0:T], AF.Exp, scale=-0.5, reads=[("scr", j)], writes=[("scr", j)])
                for j, dst in ((0, qnT), (1, knT)):
                    stt("dve", dst[:, 0:T], cv[:, j, 0:T], (128.0 ** -0.5) if j == 0 else 1.0, scr[j][:, 0:T], ALU.mult, ALU.mult, reads=[("cv", j), ("scr", j)], writes=["qnT" if j == 0 else "knT"])
                tt("dve", vbT[:, 0:T], cv[:, 2, 0:T], beta_b[:, 0:T], ALU.mult, reads=[("cv", 2), "beta_b"], writes=["vbT"])
                tt("dve", qdT[:, 0:T], qnT[:, 0:T], egc_b[:, 0:T], ALU.mult, reads=["qnT", kE], writes=[kQ])
                tt("pool", tmpf[:, 0:T], beta_b[:, 0:T], egc_b[:, 0:T], ALU.mult, reads=["beta_b", kE, ("scr", 0)], writes=[("scr", 0)])
                tt("pool", kbgT[:, 0:T], knT[:, 0:T], tmpf[:, 0:T], ALU.mult, reads=["knT", ("scr", 0)], writes=["kbgT"])
                nch = T // 64
                for ch in range(nch):
                    ts("dve", ekd_b[:, ch * 64:(ch + 1) * 64], gc_b[:, ch * 64:(ch + 1) * 64], gc_b[:, ch * 64 + 63:ch * 64 + 64], None, ALU.subtract, reads=["gc_b", ("scr", 1)], writes=[("scr", 1)])
                act(ekd_b[:, 0:T], ekd_b[:, 0:T], AF.Exp, scale=-1.0, reads=[("scr", 1)], writes=[("scr", 1)])
                tt("pool", kdT[:, 0:T], knT[:, 0:T], ekd_b[:, 0:T], ALU.mult, reads=["knT", ("scr", 1)], writes=["kdT"])
                Wl.append(sink[0])
                sink[0] = None
                Sk, Sbk = "S32_%d" % h, "Sb_%d" % h

                def prep(p):
                    pc = slice(p * 128, (p + 1) * 128)
                    s_ = p % 4
                    bi_ = (0, 1, 2, 6)[s_]
                    Bp, bk = B[bi_], ("B", bi_)
                    xb_ = s_ * 1024
                    Eb_s, EE_s, junkp_s = x32f[:, xb_:xb_ + 128], x32f[:, xb_ + 128:xb_ + 384], x32f[:, xb_ + 384:xb_ + 512]
                    bfv = x32f[:, xb_ + 512:xb_ + 1024].bitcast(BF16)
                    PPv = lambda c_: bfv[:, c_ * 256:(c_ + 1) * 256]
                    Ybv = lambda c_: bfv[:, 512 + c_ * 128:512 + (c_ + 1) * 128]
                    kbgm_s, vbm_s = bfv[:, 768:896], bfv[:, 896:1024]
                    k = lambda n: (n, s_)
                    ra, rb = slice(2 * s_ * 128, (2 * s_ + 1) * 128), slice((2 * s_ + 1) * 128, (2 * s_ + 2) * 128)
                    ka, kb = "pTr%d" % (2 * s_), "pTr%d" % (2 * s_ + 1)
                    Es = Eb_s
                    MTs, ATs = MA[:, s_, 0:128], MA[:, s_, 128:256]
                    tt("dve", junkp_s, gc_b[:, pc], identf, ALU.mult, reads=["gc_b", "cst"], writes=[k("junkp")])
                    ve("dve", lambda e, s_=s_, j_=junkp_s: e.reduce_sum(out=gcol[:, s_:s_ + 1], in_=j_, axis=AX.X), reads=[k("junkp")], writes=[k("gcol")])
                    stt("dve", Es, gc_b[:, pc], gcol[:, s_:s_ + 1], zer[:], ALU.subtract, ALU.min, reads=["gc_b", k("gcol"), "zer"], writes=[k("E")])
                    act(Es, Es, AF.Exp, reads=[k("E")], writes=[k("E")])
                    tt("dve", EE_s[:, 128:256], Es, inclT, ALU.mult, reads=[k("E"), "cst"], writes=[k("EI")])
                    tt("dve", EE_s[:, 0:128], Es, strictT, ALU.mult, reads=[k("E"), "cst"], writes=[k("EB")])
                    tt("dve", EE_s[:, 0:128], EE_s[:, 0:128], beta_b[:, pc], ALU.mult, reads=[k("EB"), "beta_b"], writes=[k("EB")])
                    mm(Bp[:, 0:128], knT[:, pc], knT[:, pc], reads=["knT"], writes=[bk])
                    mm(Bp[:, 128:256], knT[:, pc], qnT[:, pc], reads=["knT", "qnT"], writes=[bk])
                    tt("dve", MA[:, s_, :], Bp[:, 0:256], EE_s, ALU.mult, reads=[bk, k("EB"), k("EI")], writes=[k("MA")])
                    tr(pTr[:, ra], MTs, reads=[k("MA")], writes=[ka, "pTrbank"])
                    act(PPv(0)[:, 0:128], pTr[:, ra], AF.Copy, reads=["pTrbank", ka], writes=[("PP", s_, 0)])
                    tt("dve", Ybv(0), identf, MTs, ALU.subtract, reads=["cst", k("MA")], writes=[("Yb", s_, 0)])
                    cur = 0
                    for lvl in range(1, 6):
                        nx = cur ^ 1
                        Pc = PPv(cur)[:, 0:128]
                        PTc = MTs if lvl == 1 else PPv(cur)[:, 128:256]
                        rk = [("PP", s_, cur), k("MA")]
                        mm(Bp[:, 0:128], PTc, Pc, reads=rk, writes=[bk])
                        if lvl < 5:
                            mm(Bp[:, 128:256], Pc, PTc, reads=rk, writes=[bk])
                            act(PPv(nx), Bp[:, 0:256], AF.Copy, reads=[bk], writes=[("PP", s_, nx)])
                        else:
                            act(PPv(nx)[:, 0:128], Bp[:, 0:128], AF.Copy, reads=[bk], writes=[("PP", s_, nx)])
                        mm(Bp[:, 256:384], PPv(nx)[:, 0:128], Ybv(cur), reads=[("PP", s_, nx), ("Yb", s_, cur)], writes=[bk])
                        tt("dve", Ybv(nx), Bp[:, 256:384], Ybv(cur), ALU.add, reads=[bk, ("Yb", s_, cur)], writes=[("Yb", s_, nx)])
                        cur = nx
                    assert cur == 1
                    Y, Yk = Ybv(1), ("Yb", s_, 1)
                    tr(pTr[:, rb], kbgT[:, pc], reads=["kbgT"], writes=[kb, "pTrbank"])
                    act(kbgm_s, pTr[:, rb], AF.Copy, reads=["pTrbank", kb], writes=[k("kbgm")])
                    tr(pTr[:, rb], vbT[:, pc], reads=["vbT"], writes=[kb, "pTrbank"])
                    ve("dve", lambda e, v_=vbm_s, rb=rb: e.tensor_copy(out=v_, in_=pTr[:, rb]), reads=["pTrbank", kb], writes=[k("vbm")])
                    tr(pTr[:, ra], kdT[:, pc], reads=["kdT"], writes=[ka, "pTrbank"])
                    act(kdm[:, s_, :], pTr[:, ra], AF.Copy, reads=["pTrbank", ka], writes=[k("kdm")])
                    mm(Bp[:, 384:512], Y, vbm_s, reads=[Yk, k("vbm")], writes=[bk])
                    act(um[:, s_, :], Bp[:, 384:512], AF.Copy, reads=[bk], writes=[k("um")])
                    mm(Bp[:, 256:384], kbgm_s, Y, reads=[Yk, k("kbgm")], writes=[bk])
                    ve("dve", lambda e, s_=s_, Bp=Bp: e.tensor_copy(out=wTm[:, s_, :], in_=Bp[:, 256:384]), reads=[bk], writes=[k("wTm")])

                def rec(p):
                    pc = slice(p * 128, (p + 1) * 128)
                    s_ = p % 4
                    ra = slice(2 * s_ * 128, (2 * s_ + 1) * 128)
                    ka = "pTr%d" % (2 * s_)
                    k = lambda n: (n, s_)
                    for c in ((1,) if meta else (0, 1)):
                        hs = slice(c * 64, (c + 1) * 64)
                        mm(B[3][:, 0:128], wTm[:, s_, :], Sb[:, h, :], reads=[k("wTm"), Sbk], writes=[("B", 3)])
                        tt("dve", vnew[hs, :], um[hs, s_, :], B[3][hs, 0:128], ALU.subtract, reads=[k("um"), ("B", 3)], writes=["vnew"])
                        mm(B[4][:, 0:128], qdT[:, pc], Sb[:, h, :], start=True, stop=False, reads=[kQ, Sbk], writes=[("B", 4)])
                        mm(B[4][:, 0:128], MA[:, s_, 128:256], vnew[:], start=False, stop=True, reads=[k("MA"), "vnew"], writes=[("B", 4)])
                        act(ot[hs, :], B[4][hs, 0:128], AF.Copy, reads=[("B", 4)], writes=["ot"])
                        mm(B[5][:, 0:128], kdm[hs, s_, :], vnew[hs, :], reads=[k("kdm"), "vnew"], writes=[("B", 5)])
                        cdc = p * 128 + c * 64 + 63
                        stt("dve", Sb[:, h, :], S32[:, h, :], egc_b[:, cdc:cdc + 1], B[5][:, 0:128], ALU.mult, ALU.add, reads=[Sk, kE, ("B", 5)], writes=[Sbk])
                        stt("dve", S32[:, h, :], S32[:, h, :], egc_b[:, cdc:cdc + 1], B[5][:, 0:128], ALU.mult, ALU.add, reads=[Sk, kE, ("B", 5)], writes=[Sk])
                    if not meta:
                        ve("pool", lambda e: e.memset(ss[:, 5:6], 0.0), writes=["ss5"])
                        act(junk2[:], ot[:], AF.Square, accum=ss[:, 5:6], reads=["ot"], writes=["junk2", "ss5"])
                        ts("dve", ss[:, 5:6], ss[:, 5:6], 1.0 / 128, EPS, ALU.mult, ALU.add, reads=["ss5"], writes=["ss5"])
                        act(ss[:, 5:6], ss[:, 5:6], AF.Ln, reads=["ss5"], writes=["ss5"]); act(ss[:, 5:6], ss[:, 5:6], AF.Exp, scale=-0.5, reads=["ss5"], writes=["ss5"])
                        ts("dve", onb[:], ot[:], ss[:, 5:6], None, ALU.mult, reads=["ot", "ss5"], writes=["onb"])
                        tr(pTr[:, ra], onb[:], reads=["onb"], writes=[ka, "pTrbank"])
                        stt("dve", ozD_s[:, pc], pTr[:, ra], sm[:, 9:10], sdz[:, pc], ALU.mult, ALU.mult, reads=["pTrbank", ka, "sm_dnw", kZ], writes=[kO])

                npairs = T // 128
                preps, recs = [], []
                for p in range(npairs):
                    sink[0] = []
                    prep(p)
                    preps.append(sink[0])
                    sink[0] = []
                    rec(p)
                    recs.append(sink[0])
                sink[0] = None
                lst = []
                for l_ in preps:
                    lst = interleave(lst, l_)
                PRl.append(lst)
                rc_ = []
                for p in range(npairs):
                    rc_ = rc_ + recs[p]
                if not meta:
                    sink[0] = []
                    dma(cinD[sbi // NSB2][h * 128:(h + 1) * 128, (sbi % NSB2) * 256:(sbi % NSB2) * 256 + 256], ozD_s.bitcast(F32), reads=[kO], writes=[("cinD", h, sbi)])
                    rc_ = rc_ + sink[0]
                    sink[0] = None
                RCl.append(rc_)
            emit_list(interleave(EPI1, Wl[0]))
            fence()
            emit_list(interleave(PRl[0], Wl[1][:wsplit]), extra_reads=(TX,))
            emit_list(interleave(RCl[0], Wl[1][wsplit:]))
            fence()
            emit_list(PRl[1], extra_reads=(TX,))
            return RCl[1]

        def collect(c):
            kA = [("cinA", h, sj) for h in range(2) for sj in range(c * NSB2, (c + 1) * NSB2)]
            kD = [("cinD", h, sj) for h in range(2) for sj in range(c * NSB2, (c + 1) * NSB2)]
            op("pool", lambda e, c=c: e.collective_compute("AllGather", ALU.bypass, replica_groups=GROUPS, ins=[cinA_t[c].ap().opt()], outs=[coutA[c * 1024:(c + 1) * 1024, :].opt()]), reads=kA, writes=[("coutA", c)], stream="ccA")
            op("pool", lambda e, c=c: e.collective_compute("AllGather", ALU.bypass, replica_groups=GROUPS, ins=[cinD_t[c].ap().opt()], outs=[coutD[c * 1024:(c + 1) * 1024, :].opt()]), reads=kD, writes=[("coutD", c)], stream="ccD")

        pend = superblock(-1, [])
        for sbi in range(NSB):
            pend = superblock(sbi, pend)
        emit_list(pend)
        collect(NSB // NSB2 - 1)
        P.barrier()
        P.finalize(st, semstack)
        P1 = P

    NT = NT2


    with ExitStack() as st:
        sb_ = lambda n, s, d=F32: st.enter_context(nc.sbuf_tensor("q_" + n, s, d))
        Wg = sb_("Wg", [128, 8, 2048], BF16)
        Wba = sb_("Wba", [128, 8, D], BF16)
        Wbd = sb_("Wbd", [128, 8, D], BF16)
        Wo = sb_("Wo", [128, 8, D], BF16)
        x32 = sb_("x32", [128, 8, 512])
        sq = sb_("sq", [128, 2, 512], BF16)
        rstd = sb_("rstd", [128, 512])
        hnT = sb_("hnT", [128, 8, 512], BF16)
        oA = sb_("oA", [128, 8, 512], BF16)
        oD = sb_("oD", [128, 8, 512], BF16)
        sg = sb_("sg", [128, 16, 512])
        tA = sb_("tA", [128, 512])
        mT = sb_("mT", [128, 8, 512], BF16)
        xt = sb_("xt", [128, D])
        h2 = sb_("h2", [128, D])
        yo = sb_("yo", [128, D])
        junk = sb_("junk", [128, D])
        fw = sb_("fw", [128, D])
        nw = sb_("nw", [128, 8])
        onesb = sb_("onesb", [128, 128], BF16)
        ss = sb_("ss", [128, 4])
        B = [st.enter_context(nc.psum_tensor("QB%d" % i, [128, 512], F32)) for i in range(6)]
        P = Prog(nc, "p2")
        op = P.op
        dq = [0]

        dlast = {}

        def dma(out, in_, reads=(), writes=(), q=None):
            q = q or "sp"
            dq[0] += 1
            name = "%s:%d" % (q, dq[0] % 8)
            prev = dlast.get(name)
            o_ = op(q, lambda e, o=out, i=in_: e.dma_start(out=o, in_=i), reads=reads, writes=writes, stream=name, after=([prev] if prev is not None else ()))
            dlast[name] = o_
            return o_

        def mm(out, lhsT, rhs, start=True, stop=True, reads=(), writes=()):
            return op("pe", lambda e, o=out, l=lhsT, r=rhs, s0=start, s1=stop: e.matmul(o, lhsT=l, rhs=r, start=s0, stop=s1, skip_group_check=True), reads=reads, writes=writes)

        def act(out, in_, func, scale=1.0, accum=None, reads=(), writes=()):
            def f(e, o=out, i=in_, fn=func, s=scale, a=accum):
                kw = {}
                if a is not None:
                    kw["accum_out"] = a
                return e.activation(out=o, in_=i, func=fn, scale=s, **kw)
            return op("act", f, reads=reads, writes=writes)

        def tt(eng, out, a, b, alu, reads=(), writes=()):
            return op(eng, lambda e, o=out, x=a, y_=b, u=alu: e.tensor_tensor(out=o, in0=x, in1=y_, op=u), reads=reads, writes=writes)

        def ts(eng, out, a, s1, s2, op0, op1=None, reads=(), writes=()):
            def f(e, o=out, x=a, p=s1, q=s2, u=op0, v=op1):
                if v is None:
                    return e.tensor_scalar(out=o, in0=x, scalar1=p, scalar2=None, op0=u)
                return e.tensor_scalar(out=o, in0=x, scalar1=p, scalar2=q, op0=u, op1=v)
            return op(eng, f, reads=reads, writes=writes)

        def stt(eng, out, a, s, b, op0, op1, reads=(), writes=()):
            return op(eng, lambda e, o=out, x=a, p=s, y_=b, u=op0, v=op1: e.scalar_tensor_tensor(out=o, in0=x, scalar=p, in1=y_, op0=u, op1=v), reads=reads, writes=writes)

        dma(nw[:], normw[:, :], writes=["nw"])
        dma(fw[:], fnw.partition_broadcast(128), writes=["fw"])
        op("dve", lambda e: e.memset(onesb[:], 1.0), writes=["onesb"])
        x32f = x32[:].rearrange("p a b -> p (a b)")
        n = 0
        sgf = sg[:].rearrange("p a b -> p (a b)")
        for (src, dst, ncol, fold) in ((wg, Wg, 2048, True), (wba, Wba, D, False), (wbd, Wbd, D, False), (wo, Wo, D, False)):
            for kc in range(8):
                for c in range(0, ncol, 2048):
                    w_ = min(2048, ncol - c)
                    sl_ = n % 6
                    if sl_ < 2:
                        stg_, hb = x32f, sl_ * 2048
                        sk = [("x", 4 * sl_ + i) for i in range(4)]
                    else:
                        stg_, hb = sgf, (sl_ - 2) * 2048
                        sk = [("sg", 4 * (sl_ - 2) + i) for i in range(4)]
                    dma(stg_[:, hb:hb + w_], src[kc * 128:(kc + 1) * 128, c:c + w_], writes=sk, q=("sp", "pool", "act")[n % 3])
                    if n % 2 == 0:
                        if fold:
                            ts("dve", dst[:, kc, c:c + w_], stg_[:, hb:hb + w_], nw[:, kc:kc + 1], None, ALU.mult, reads=sk + ["nw"], writes=[id(dst)])
                        else:
                            op("dve", lambda e, d_=dst, kc=kc, c=c, w_=w_, hb=hb, stg_=stg_: e.tensor_copy(out=d_[:, kc, c:c + w_], in_=stg_[:, hb:hb + w_]), reads=sk, writes=[id(dst)])
                    else:
                        sc_ = nw[:, kc:kc + 1] if fold else 1.0
                        op("act", lambda e, d_=dst, kc=kc, c=c, w_=w_, hb=hb, sc_=sc_, stg_=stg_: e.activation(out=d_[:, kc, c:c + w_], in_=stg_[:, hb:hb + w_], func=AF.Copy, scale=sc_), reads=sk + ["nw"], writes=[id(dst)])
                    n += 1
        outs = []
        pidc = {}

        def rq(e):
            if 'r' not in pidc:
                pidc['r'] = e.partition_id() % 4
            return pidc['r']

        ccw = [(P1.streams[n_].sem, P1.streams[n_].ninc) for n_ in ("ccA", "ccD")]
        op("sp", lambda e: e.dma_start(out=ownA[:, :], in_=coutA[bass.ts(rq(e), 1024), :]), writes=["ownA"], stream="g:0", ext=ccw)
        op("sp", lambda e: e.dma_start(out=ownD[:, :], in_=coutD[bass.ts(rq(e), 1024), :]), writes=["ownD"], stream="g:1", ext=ccw)
        for s in range(NSB2):
            c0 = s * 512
            for kc in range(8):
                dma(x32[:, kc, :], xT2[kc * 128:(kc + 1) * 128, c0:c0 + 512], writes=[("x", kc)], q=("sp" if kc % 2 == 0 else "pool"))
                dma(oA[:, kc, :].bitcast(F32), ownA[kc * 128:(kc + 1) * 128, s * 256:(s + 1) * 256], reads=["ownA"], writes=[("oA", kc)])
                dma(oD[:, kc, :].bitcast(F32), ownD[kc * 128:(kc + 1) * 128, s * 256:(s + 1) * 256], reads=["ownD"], writes=[("oD", kc)], q="pool")
            for kc in range(8):
                act(sq[:, kc % 2, :], x32[:, kc, :], AF.Square, reads=[("x", kc)], writes=[("sq", kc % 2)])
                mm(B[0][:, :], onesb[:], sq[:, kc % 2, :], start=(kc == 0), stop=(kc == 7), reads=["onesb", ("sq", kc % 2)], writes=[("B", 0)])
            ts("dve", rstd[:], B[0][:, :], 1.0 / D, EPS, ALU.mult, ALU.add, reads=[("B", 0)], writes=["rstd"])
            act(rstd[:], rstd[:], AF.Ln, reads=["rstd"], writes=["rstd"]); act(rstd[:], rstd[:], AF.Exp, scale=-0.5, reads=["rstd"], writes=["rstd"])
            for kc in range(8):
                tt("dve" if kc % 2 == 0 else "pool", hnT[:, kc, :], x32[:, kc, :], rstd[:], ALU.mult, reads=[("x", kc), "rstd"], writes=["hnT"])
            for c in range(16):
                b = c % 2
                for kc in range(8):
                    mm(B[b][:, :], Wg[:, kc, c * 128:(c + 1) * 128], hnT[:, kc, :], start=(kc == 0), stop=(kc == 7), reads=[id(Wg), "hnT"], writes=[("B", b)])
                act(sg[:, c, :], B[b][:, :], AF.Sigmoid, reads=[("B", b)], writes=[("sg", c)])
            for dc in range(8):
                ba_, bd_ = (2, 3) if dc % 2 == 0 else (4, 5)
                for kc in range(8):
                    mm(B[ba_][:, :], Wba[:, kc, dc * 128:(dc + 1) * 128], oA[:, kc, :], start=(kc == 0), stop=(kc == 7), reads=[id(Wba), ("oA", kc)], writes=[("B", ba_)])
                for kc in range(8):
                    mm(B[bd_][:, :], Wbd[:, kc, dc * 128:(dc + 1) * 128], oD[:, kc, :], start=(kc == 0), stop=(kc == 7), reads=[id(Wbd), ("oD", kc)], writes=[("B", bd_)])
                tt("dve", tA[:], B[ba_][:, :], sg[:, dc, :], ALU.mult, reads=[("B", ba_), ("sg", dc)], writes=["tA"])
                tt("dve", junk[:, 0:512], B[bd_][:, :], sg[:, 8 + dc, :], ALU.mult, reads=[("B", bd_), ("sg", 8 + dc)], writes=["junk5"])
                tt("dve", mT[:, dc, :], junk[:, 0:512], tA[:], ALU.add, reads=["junk5", "tA"], writes=["mT"])
            for j in range(4):
                r0 = c0 + j * 128
                dma(xt[:], xtok[r0:r0 + 128, :], writes=["xt"])
                for hc in range(2):
                    bo_ = (j % 2) * 2 + hc
                    for dc in range(8):
                        mm(B[bo_][:, :], mT[:, dc, j * 128:(j + 1) * 128], Wo[:, dc, hc * 512:(hc + 1) * 512], start=(dc == 0), stop=(dc == 7), reads=["mT", id(Wo)], writes=[("B", bo_)])
                    tt("dve", h2[:, hc * 512:(hc + 1) * 512], B[bo_][:, :], xt[:, hc * 512:(hc + 1) * 512], ALU.add, reads=[("B", bo_), "xt"], writes=["h2"])
                op("pool", lambda e: e.memset(ss[:, 0:1], 0.0), writes=["ss0"])
                act(junk[:], h2[:], AF.Square, accum=ss[:, 0:1], reads=["h2"], writes=["junk5", "ss0"])
                ts("dve", ss[:, 0:1], ss[:, 0:1], 1.0 / D, EPS, ALU.mult, ALU.add, reads=["ss0"], writes=["ss0"])
                act(ss[:, 0:1], ss[:, 0:1], AF.Ln, reads=["ss0"], writes=["ss0"]); act(ss[:, 0:1], ss[:, 0:1], AF.Exp, scale=-0.5, reads=["ss0"], writes=["ss0"])
                stt("dve", yo[:], h2[:], ss[:, 0:1], fw[:], ALU.mult, ALU.mult, reads=["h2", "ss0", "fw"], writes=["yo"])
                outs.append(dma(y[r0:r0 + 128, :], yo[:], reads=["yo"]))
        P.final_wait("sp", outs)
        P.finalize(st, semstack)
    semstack.close()
    return nc


def _consts():
    c = np.zeros((128, 1024), np.float32)
    j = np.arange(128)[:, None]
    i = np.arange(128)[None, :]
    same = (j // 64) == (i // 64)
    c[:, 0:128] = np.eye(128, dtype=np.float32)
    c[:, 128:256] = ((i >= j) & same)
    c[:, 256:384] = ((i > j) & same)
    c[:, 384:512] = (i >= j)
    c[:, 512:1024] = (np.arange(512) % 64 != 0)[None, :]
    return c


def kernel(x, meta_tokens, norm_w, w_in, lambda_q1, lambda_k1, lambda_q2, lambda_k2,
           attn_norm_w, conv_w, a_log, dt_bias, dn_norm_w, w_branch_attn, w_branch_delta,
           w_out, final_norm_w):
    f = lambda a: np.ascontiguousarray(np.asarray(a, dtype=np.float32))
    x = f(x)
    Bn, S, _ = x.shape
    NSB = S // 512
    w = f(w_in)[0]
    offs = np.cumsum([0, 1024, 1024, 1024, 1024, 1024, 1024, 1024, 1024, 8, 8, 1024, 1024])
    seg = lambda i: w[:, offs[i]:offs[i + 1]]
    aq, ak, av, az, dq_, dk_, dv_, dz, db, da, ga, gd = [seg(i) for i in range(12)]
    cwf = f(conv_w)[0]
    nwl = f(norm_w)[0].reshape(8, 128).T.copy()
    metaT = np.zeros((D, 128), np.float32)
    metaT[:, 112:] = f(meta_tokens).T
    lamv = np.concatenate([f(lambda_q1)[0], f(lambda_k1)[0], f(lambda_q2)[0], f(lambda_k2)[0]])[None, :].copy()
    consts = _consts()
    xTs = [np.ascontiguousarray(x[b].T) for b in range(Bn)]
    maps1 = []
    for core in range(8):
        b, hp = core // 4, core % 4
        hs = [2 * hp, 2 * hp + 1]
        col = lambda m, h: m[:, h * 128:(h + 1) * 128]
        wfm = np.concatenate([col(m, h) for m in (aq, ak, az, dq_, dk_, dv_, dz) for h in hs], axis=1)
        wbgl = np.stack([db[:, hs[0]], db[:, hs[1]], da[:, hs[0]], da[:, hs[1]]], axis=1)
        wavl = np.concatenate([col(av, h) for h in hs], axis=1)
        cwl = np.zeros((128, 24), np.float32)
        for j in range(3):
            for hh in range(2):
                ch0 = j * 1024 + hs[hh] * 128
                cwl[:, (2 * j + hh) * 4:(2 * j + hh) * 4 + 4] = cwf[:, ch0:ch0 + 128].T
        maps1.append(dict(
            xT=xTs[b], metaT=metaT, wfm=np.ascontiguousarray(wfm), wbg=np.ascontiguousarray(wbgl),
            wav=np.ascontiguousarray(wavl), normw=nwl, convw=cwl,
            alog=f(a_log)[0][hs][None, :].copy(), dtb=f(dt_bias)[0][hs][None, :].copy(), lamv=lamv,
            anw=f(attn_norm_w)[0][:, None].copy(), dnw=f(dn_norm_w)[0][:, None].copy(), consts=consts))
    NQ = S // 4
    wgl = np.ascontiguousarray(np.concatenate([ga, gd], axis=1))
    for core in range(8):
        b, r = core // 4, core % 4
        sl = slice(r * NQ, (r + 1) * NQ)
        maps1[core].update(dict(
            xT2=np.ascontiguousarray(xTs[b][:, sl]), xtok=np.ascontiguousarray(x[b, sl]),
            wg=wgl, wba=f(w_branch_attn)[0], wbd=f(w_branch_delta)[0], wo=f(w_out)[0],
            fnw=f(final_norm_w)[None, :].copy()))
    nc = build_fused(NSB)
    res = run_bass_kernel_spmd(nc, maps1, core_ids=list(range(8))).results
    out = np.zeros((Bn, S, D), np.float32)
    for core in range(8):
        b, r = core // 4, core % 4
        out[b, r * NQ:(r + 1) * NQ] = np.asarray(res[core]["y"])
    return out
```
